# Optimizing a Trainium2 kernel written in Bass

```python
import math
import jax, jax.numpy as jnp
from jax import lax
import numpy as np

D_MODEL = 2048
BATCH = 32
SEQ = 256
DEPTH = 2
DEC_BATCH = 2
DEC_SEQ = 1024
PAST_LEN = 256

GRID_W = 64
D_MIX = D_MODEL
D_A = D_MIX // 2
D_B = D_MIX // 4
D_C = D_MIX - D_A - D_B
LRU_BLOCKS = 8
LRU_BW = D_A // LRU_BLOCKS
LRU_C = 8.0
CONV_A_W = 4
FNET_GROUPS = 4
FNET_GW = D_B // FNET_GROUPS
CONV_C_W = 31
D_IN = 2 * D_A + D_B + 2 * D_C
D_FF = ((8 * D_MODEL + 3 * 256 - 1) // (3 * 256)) * 256
N_MOD = 6
EPS = 1e-6

kernel_name = "hybrid_lru_fnet_conformer_diffusion_step"


def _rmsnorm(x, g):
    xf = x.astype(jnp.float32)
    y = xf * lax.rsqrt(jnp.mean(xf * xf, axis=-1, keepdims=True) + EPS)
    return (y * g.astype(jnp.float32)).astype(x.dtype)


def _dwconv(x, w, b, pad_left, pad_right):
    C = x.shape[-1]
    y = lax.conv_general_dilated(x, w[:, None, :].astype(x.dtype), window_strides=(1,),
                                 padding=[(pad_left, pad_right)],
                                 dimension_numbers=('NWC', 'WIO', 'NWC'),
                                 feature_group_count=C)
    return y + b.astype(x.dtype)


def _lin_scan(a, u, h0, reverse):
    def op(l, r):
        return (l[0] * r[0], r[0] * l[1] + r[1])
    a_c, u_c = lax.associative_scan(op, (a, u), reverse=reverse, axis=1)
    return a_c * h0[:, None, :] + u_c


def _rg_lru(xa, h0, wa, ba, wx, bx, lam):
    B_, S_, _ = xa.shape
    xblk = xa.reshape(B_, S_, LRU_BLOCKS, LRU_BW)
    f32 = jnp.float32
    gate_a = jnp.einsum('bsnk,dnkj->dbsnj', xblk, wa.astype(f32)).reshape(2, B_, S_, D_A) + ba.astype(f32)[:, None, None, :]
    gate_x = jnp.einsum('bsnk,dnkj->dbsnj', xblk, wx.astype(f32)).reshape(2, B_, S_, D_A) + bx.astype(f32)[:, None, None, :]
    log_a = -LRU_C * jax.nn.sigmoid(gate_a) * jax.nn.softplus(-lam.astype(f32))[:, None, None, :]
    a = jnp.exp(log_a)
    u = jnp.sqrt(-jnp.expm1(2.0 * log_a)) * jax.nn.sigmoid(gate_x) * xa[None]
    h_f = _lin_scan(a[0], u[0], h0[:, 0], False)
    h_b = _lin_scan(a[1], u[1], h0[:, 1], True)
    h_last = jnp.stack([h_f[:, -1], h_b[:, 0]], axis=1)
    return h_f + h_b, h_last


def _fourier(xb):
    B_, S_, _ = xb.shape
    z = xb.astype(jnp.float32).reshape(B_, S_, FNET_GROUPS, FNET_GW)
    z = jnp.fft.fft2(z, axes=(1, 3), norm='ortho').real
    return z.reshape(B_, S_, D_B)


def _conformer_conv(xc, w, b, g, beta):
    v = xc[..., :D_C] * jax.nn.sigmoid(xc[..., D_C:])
    half = (CONV_C_W - 1) // 2
    v = _dwconv(v, w, b, half, half).astype(jnp.float32)
    mu = jnp.mean(v, axis=-1, keepdims=True)
    var = jnp.mean(jnp.square(v - mu), axis=-1, keepdims=True)
    v = (v - mu) * lax.rsqrt(var + EPS) * g.astype(jnp.float32) + beta.astype(jnp.float32)
    return jax.nn.silu(v).astype(xc.dtype)


def _grid_pos_embed(L, rows, dtype):
    t = jnp.arange(rows * GRID_W)
    r = (t // GRID_W).astype(jnp.float32)
    col = (t % GRID_W).astype(jnp.float32)
    nf = D_MODEL // 4
    omega = 1.0 / (10000.0 ** (jnp.arange(nf, dtype=jnp.float32) / nf))
    def enc(p):
        ang = p[:, None] * omega[None, :]
        return jnp.concatenate([jnp.sin(ang), jnp.cos(ang)], axis=-1)
    return jnp.concatenate([enc(r), enc(col)], axis=-1).astype(dtype)


def _layer(x, mod, h0, norm_mix, norm_ffn, w_in, conv_a_w, conv_a_b, lru_wa, lru_ba, lru_wx,
           lru_bx, lru_lam, conv_c_w, conv_c_b, ln_c_g, ln_c_b, out_norm, w_out, w_gu, w_down):
    shift1, scale1, gate1, shift2, scale2, gate2 = [mod[:, i][:, None, :] for i in range(N_MOD)]
    h = _rmsnorm(x, norm_mix) * (1 + scale1) + shift1
    u = h @ w_in
    xa, ya, xb, xc = jnp.split(u, [D_A, 2 * D_A, 2 * D_A + D_B], axis=-1)
    xa = _dwconv(xa, conv_a_w, conv_a_b, 2, 1)
    ha, h_last = _rg_lru(xa.astype(jnp.float32), h0, lru_wa, lru_ba, lru_wx, lru_bx, lru_lam)
    out_a = (ha * jax.nn.gelu(ya.astype(jnp.float32))).astype(x.dtype)
    out_b = _fourier(xb).astype(x.dtype)
    out_c = _conformer_conv(xc, conv_c_w, conv_c_b, ln_c_g, ln_c_b)
    g_a, g_b, g_c = jnp.split(out_norm, [D_A, D_A + D_B])
    m = jnp.concatenate([_rmsnorm(out_a, g_a), _rmsnorm(out_b, g_b), _rmsnorm(out_c, g_c)], axis=-1)
    x = x + gate1 * (m @ w_out)
    h = _rmsnorm(x, norm_ffn) * (1 + scale2) + shift2
    g_, up = jnp.split(h @ w_gu, 2, axis=-1)
    x = x + gate2 * ((jax.nn.silu(g_) * up) @ w_down)
    return x, h_last


def setup_inputs(seed: int = 0) -> dict:
    key = jax.random.key(seed)
    ks = jax.random.split(key, 32)
    f32 = jnp.float32
    def nrm(k, shape, scale):
        return jax.random.normal(k, shape, f32) * scale
    a0 = jax.random.uniform(ks[12], (DEPTH, 2, D_A), f32, minval=0.9, maxval=0.999)
    return {
        "x_prompt": nrm(ks[0], (BATCH, SEQ, D_MODEL), 1.0),
        "x_sample": nrm(ks[1], (DEC_BATCH, DEC_SEQ, D_MODEL), 1.0),
        "c": nrm(ks[2], (DEC_BATCH, D_MODEL), 1.0),
        "state_lru": nrm(ks[3], (DEC_BATCH, DEPTH, 2, D_A), 1.0),
        "c_ctx": nrm(ks[4], (D_MODEL,), 1.0),
        "w_mod": nrm(ks[5], (DEPTH, D_MODEL, N_MOD * D_MODEL), 0.5 * D_MODEL ** -0.5),
        "b_mod": nrm(ks[6], (DEPTH, N_MOD * D_MODEL), 0.02),
        "norm_mix": 1.0 + nrm(ks[7], (DEPTH, D_MODEL), 0.02),
        "norm_ffn": 1.0 + nrm(ks[8], (DEPTH, D_MODEL), 0.02),
        "w_in": nrm(ks[9], (DEPTH, D_MODEL, D_IN), D_MODEL ** -0.5),
        "conv_a_w": nrm(ks[10], (DEPTH, CONV_A_W, D_A), CONV_A_W ** -0.5),
        "conv_a_b": nrm(ks[11], (DEPTH, D_A), 0.02),
        "lru_wa": nrm(ks[13], (DEPTH, 2, LRU_BLOCKS, LRU_BW, LRU_BW), LRU_BW ** -0.5),
        "lru_ba": nrm(ks[14], (DEPTH, 2, D_A), 0.02),
        "lru_wx": nrm(ks[15], (DEPTH, 2, LRU_BLOCKS, LRU_BW, LRU_BW), LRU_BW ** -0.5),
        "lru_bx": nrm(ks[16], (DEPTH, 2, D_A), 0.02),
        "lru_lam": jnp.log(a0) - jnp.log1p(-a0),
        "conv_c_w": nrm(ks[17], (DEPTH, CONV_C_W, D_C), CONV_C_W ** -0.5),
        "conv_c_b": nrm(ks[18], (DEPTH, D_C), 0.02),
        "ln_c_g": 1.0 + nrm(ks[19], (DEPTH, D_C), 0.02),
        "ln_c_b": nrm(ks[20], (DEPTH, D_C), 0.02),
        "out_norm": 1.0 + nrm(ks[21], (DEPTH, D_MIX), 0.02),
        "w_out": nrm(ks[22], (DEPTH, D_MIX, D_MODEL), D_MIX ** -0.5),
        "w_gu": nrm(ks[23], (DEPTH, D_MODEL, 2 * D_FF), D_MODEL ** -0.5),
        "w_down": nrm(ks[24], (DEPTH, D_FF, D_MODEL), D_FF ** -0.5),
        "final_norm": 1.0 + nrm(ks[25], (D_MODEL,), 0.02),
    }


def reference(x_prompt, x_sample, c, state_lru, c_ctx, w_mod, b_mod, norm_mix, norm_ffn, w_in,
              conv_a_w, conv_a_b, lru_wa, lru_ba, lru_wx, lru_bx, lru_lam, conv_c_w, conv_c_b,
              ln_c_g, ln_c_b, out_norm, w_out, w_gu, w_down, final_norm):
    L = x_sample.shape[1]
    ROWS = L // GRID_W
    xs = x_sample + _grid_pos_embed(L, ROWS, x_sample.dtype)[None]
    xp = x_prompt
    h0_ctx = jnp.zeros((x_prompt.shape[0], 2, D_A), jnp.float32)
    ctx_states = []
    for l in range(DEPTH):
        p = (norm_mix[l], norm_ffn[l], w_in[l], conv_a_w[l], conv_a_b[l], lru_wa[l], lru_ba[l],
             lru_wx[l], lru_bx[l], lru_lam[l], conv_c_w[l], conv_c_b[l], ln_c_g[l], ln_c_b[l],
             out_norm[l], w_out[l], w_gu[l], w_down[l])
        mod_ctx = (jax.nn.silu(c_ctx)[None] @ w_mod[l] + b_mod[l]).reshape(1, N_MOD, D_MODEL)
        mod_lat = (jax.nn.silu(c) @ w_mod[l] + b_mod[l]).reshape(-1, N_MOD, D_MODEL)
        xp, h_last = _layer(xp, mod_ctx, h0_ctx, *p)
        ctx_states.append(h_last.astype(x_prompt.dtype))
        xs, _ = _layer(xs, mod_lat, state_lru[:, l].astype(jnp.float32), *p)
    y_prompt = _rmsnorm(xp, final_norm)
    y_sample = _rmsnorm(xs, final_norm)
    new_state_lru = jnp.stack(ctx_states, axis=1)
    return (y_prompt, y_sample, new_state_lru)
```

```python
import math
import numpy as np
import concourse.bass as bass
import concourse.mybir as mybir
from concourse.bass_utils import run_bass_kernel_spmd

F32 = mybir.dt.float32
BF16 = mybir.dt.bfloat16
AF = mybir.ActivationFunctionType
ALU = mybir.AluOpType

D = 2048
NT = 1280
SEG = 256
NSEG = 5
L = 2
D_A, D_B, D_C = 1024, 512, 512
D_IN = 3584
D_FF = 5632
NHT = D_FF // 128
EPS = 1e-6
CH = [(0, 512), (512, 512), (1024, 256)]
CHSEG = [(0, 2), (2, 2), (4, 1)]
CJ = [0, 0, 1]
BLK = 4096
NSLOT = 2
SAME_ENGINE_SYNC = True

PV_LAYER = [("nm", 16), ("nf", 16), ("caw", 32), ("cab", 8), ("ba", 16), ("bx", 16), ("lam", 16),
            ("ccw", 124), ("ccb", 4), ("lng", 4), ("lnb", 4), ("on", 16), ("bmod", 96)]
PV_OFF = {}
_o = 0
for _l in range(L):
    for _n, _w in PV_LAYER:
        PV_OFF[(_n, _l)] = _o
        _o += _w
PV_OFF[("fn", 0)] = _o
_o += 16
NPV = _o


def plan():
    blocks = []
    for b in range(16):
        blocks.append(("mod", 0, b))
    for l in range(L):
        for n in range(8):
            blocks.append(("inA", l, n))
            if l == 0 and n < 4:
                blocks.append(("mod", 0, 16 + 2 * n))
                blocks.append(("mod", 0, 17 + 2 * n))
        bq = [24]
        if l == 0:
            blocks.append(("b_begin", l))
        for q in range(4):
            blocks.append(("outA", l, q))
            if l == 0:
                for _ in range(2):
                    blocks.append(("mod", 0, bq[0]))
                    bq[0] += 1
        if l == 0:
            blocks.append(("b_end", l))

        def bmods():
            if l == 0:
                blocks.append(("mod", 0, bq[0]))
                bq[0] += 1
        if l == 0:
            blocks.append(("b_begin", l))
        for b in range(2):
            blocks.append(("inB", l, b))
            bmods()
        for h in range(2):
            blocks.append(("P1c", h))
            bmods()
            blocks.append(("P1ns", h))
            bmods()
        for q in range(2):
            blocks.append(("outB", l, q))
            bmods()
        if l == 0:
            blocks.append(("b_end", l))
        for j in range(4):
            blocks.append(("inC", l, j))
        for q in range(2):
            blocks.append(("outC", l, q))
        blocks.append(("ffn_begin", l))
        for q in range(6):
            nk = 8 if q < 5 else 4
            for tl in range(nk):
                t = q * 8 + tl
                blocks.append(("gu", l, t))
                if l == 0:
                    if t < 8:
                        blocks.append(("mod", 0, 40 + t))
                    blocks.append(("mod", 1, t))
                    if t == NHT - 1:
                        for b in range(NHT, 48):
                            blocks.append(("mod", 1, b))
            for r in range(4):
                blocks.append(("down", l, q, r))
        blocks.append(("ffn_end", l))
    return blocks


def plan_phased():
    out, ph = [], []
    cur = 0
    for b in plan():
        if b[0] == "ffn_begin":
            cur = b[1] + 1
        elif b[0] == "b_begin":
            cur = 3
        elif b[0] in ("ffn_end", "b_end"):
            cur = 0
        else:
            out.append(b)
            ph.append(cur)
    return out, ph


class Sched:
    ENG = ("pe", "act", "dve", "pool", "sp")

    def __init__(self, nc):
        self.nc = nc
        self.prog = {e: [] for e in self.ENG}
        self.count = {e: 0 for e in self.ENG}
        self.seen = {e: {} for e in self.ENG}
        self.lastw = {}
        self.readers = {}
        self.dmacount = {}
        self.semnames = list(self.ENG)
        self.nbank = 0
        self.ring = list(range(7))
        self.phase_id = 0
        self.phase_tokens = []
        self.phase_seen = {}

    def _need(self, e, toks):
        need = {}
        for tok in toks:
            if tok is None:
                continue
            s, v = tok
            if s == e and (e in ("pe", "sp") or not SAME_ENGINE_SYNC):
                continue
            if self.seen[e].get(s, 0) >= v:
                continue
            if need.get(s, 0) < v:
                need[s] = v
        for s, v in need.items():
            self.seen[e][s] = v
            self.prog[e].append(("wait", s, v))

    @staticmethod
    def _is_scratch(key):
        k0 = key[0]
        if not isinstance(k0, str):
            return False
        if k0 == "tmpf":
            return True
        if k0 == "w":
            return len(key) > 1 and key[1] in (2, 3)
        return len(k0) >= 2 and k0[0] in "ABCF" and k0[1].isdigit()

    def _deps(self, e, reads, writes):
        toks = []
        for r in reads:
            toks.append(self.lastw.get(r))
        for w in writes:
            if self._is_scratch(w) and self.phase_seen.get(w) != self.phase_id:
                self.phase_seen[w] = self.phase_id
                toks.extend(self.phase_tokens)
            toks.append(self.lastw.get(w))
            for t in self.readers.get(w, ()):
                if t[0] != e:
                    toks.append(t)
        self._need(e, toks)

    def _commit(self, tok, reads, writes):
        for r in reads:
            self.readers.setdefault(r, []).append(tok)
        for w in writes:
            self.lastw[w] = tok
            self.readers[w] = []

    def op(self, e, fn, reads=(), writes=()):
        self._deps(e, reads, writes)
        self.count[e] += 1
        tok = (e, self.count[e])
        self.prog[e].append(("inst", fn, e, 1))
        self._commit(tok, reads, writes)
        return tok

    def dma(self, q, fn, semkey, reads=(), writes=()):
        sname = "d_" + semkey
        if sname not in self.dmacount:
            self.dmacount[sname] = 0
            self.semnames.append(sname)
        self._deps(q, reads, writes)
        self.dmacount[sname] += 16
        tok = (sname, self.dmacount[sname])
        self.prog[q].append(("inst", fn, sname, 16))
        self._commit(tok, reads, writes)
        return tok

    def barrier(self):
        engs = ("pe", "act", "dve", "pool")
        self.phase_id += 1
        self.phase_tokens = [(f, self.count[f]) for f in engs if self.count[f] > 0]
        self.phase_tokens += [(k, v) for k, v in self.dmacount.items() if k.startswith("d_w")]

    def hard_barrier(self):
        engs = ("pe", "act", "dve", "pool")
        for e in engs:
            self._need(e, [(f, self.count[f]) for f in engs if f != e and self.count[f] > 0])

    def bank(self):
        b = self.ring[self.nbank % len(self.ring)]
        self.nbank += 1
        return b

    def emit(self, final_waits):
        nc = self.nc
        engobj = {"pe": "tensor", "act": "scalar", "dve": "vector", "pool": "gpsimd", "sp": "sync"}
        from contextlib import ExitStack
        with ExitStack() as es:
            sems = {}
            for s in self.semnames:
                sems[s] = es.enter_context(nc.semaphore("s_" + s))
            block = es.enter_context(nc.Block())

            def replay(eng, items, extra=()):
                for it in items:
                    if it[0] == "wait":
                        eng.wait_ge(sems[it[1]], it[2])
                    else:
                        ins = it[1](eng)
                        ins.then_inc(sems[it[2]], it[3])
                for s, v in extra:
                    eng.wait_ge(sems[s], v)

            @block.tensor
            def _(eng):
                replay(eng, self.prog["pe"])

            @block.scalar
            def _(eng):
                replay(eng, self.prog["act"])

            @block.vector
            def _(eng):
                replay(eng, self.prog["dve"])

            @block.gpsimd
            def _(eng):
                replay(eng, self.prog["pool"])

            @block.sync
            def _(eng):
                replay(eng, self.prog["sp"], final_waits)


class StopBuild(Exception):
    pass


def build_program(stop=None):
    import os
    ga_stop = os.environ.get("GA_STOP")

    def dbg(step):
        if ga_stop is not None and ga_stop == step:
            raise StopBuild()

    nc = bass.Bass("TRN2", target_bir_lowering=False)
    blocks, bphase = plan_phased()
    NB = len(blocks)
    nmain = sum(1 for b in blocks if b[0] not in ("P1c", "P1ns"))

    dr = {}
    dr["xT"] = nc.dram_tensor("xT", [128, 16, NT], F32, kind="ExternalInput").ap()
    dr["pos"] = nc.dram_tensor("pos", [128, 16, 1024], F32, kind="ExternalInput").ap()
    dr["cv"] = nc.dram_tensor("cv", [128, 32], F32, kind="ExternalInput").ap()
    dr["h0"] = nc.dram_tensor("h0", [128, L * 2 * 8 * NSEG], F32, kind="ExternalInput").ap()
    dr["mask"] = nc.dram_tensor("mask", [128, 1], F32, kind="ExternalInput").ap()
    dr["pv"] = nc.dram_tensor("pv", [128, NPV], F32, kind="ExternalInput").ap()
    dr["ws"] = nc.dram_tensor("ws", [nmain, 128, BLK], F32, kind="ExternalInput").ap()
    dr["pstr"] = nc.dram_tensor("pstr", [4, 128, BLK], F32, kind="ExternalInput").ap()
    dr["dsm"] = nc.dram_tensor("dsm", [128, 1408], F32, kind="ExternalInput").ap()
    dr["lruw"] = nc.dram_tensor("lruw", [L * 8, 128, 512], F32, kind="ExternalInput").ap()
    dr["yT"] = nc.dram_tensor("yT", [128, 16, NT], F32, kind="ExternalOutput").ap()
    dr["st"] = nc.dram_tensor("st", [128, L * 2 * 8 * NSEG], F32, kind="ExternalOutput").ap()

    from contextlib import ExitStack
    with ExitStack() as es:
        def sb(name, shape, dt):
            return es.enter_context(nc.sbuf_tensor("sb_" + name, shape, dt))

        x = sb("x", [128, 16, NT], F32)
        hb = sb("hb", [128, 16, NT], BF16)
        wr = sb("wr", [128, NSLOT, BLK], BF16)
        SCR_BYTES = 53824
        scr = sb("scr", [128, SCR_BYTES // 4], F32)
        pv = sb("pv", [128, NPV], F32)
        modv = sb("modv", [128, L, 96, 2], F32)
        Amod = sb("Amod", [128, L, 2, 16, 2], F32)
        lrup = sb("lrup", [128, L, 3, 16], F32)
        cvt = sb("cvt", [128, 32], F32)
        cvth = sb("cvth", [128, 32], F32)
        scv = sb("scv", [128, 32], BF16)
        h0t = sb("h0t", [128, L * 2 * 8 * NSEG], F32)
        stout = sb("stout", [128, L * 2 * 8 * NSEG], F32)
        maskt = sb("maskt", [128, 1], F32)
        cst = sb("cst", [128, 4], F32)
        ones = sb("ones", [128, 128], BF16)
        dsm = sb("dsm", [128, 1408], BF16)
        lruwb = sb("lruwb", [128, 2, 512], BF16)
        sqb = sb("sqb", [128, 2, 512], BF16)
        rstd = sb("rstd", [128, NT], F32)
        initb = sb("initb", [128, 16], F32)
        assert True
        ps = es.enter_context(nc.psum_tensor("ps", [128, 8, 512], F32))

        S = Sched(nc)

        class Carve:
            def __init__(self):
                self.off = 0

            def take(self, shape, dt):
                n = int(np.prod(shape))
                nbytes = n * (4 if dt == F32 else 2)
                nwords = (nbytes + 3) // 4
                assert (self.off + nwords) * 4 <= SCR_BYTES, (self.off, nwords)
                ap = scr[:, self.off:self.off + nwords]
                self.off += nwords
                if dt == BF16:
                    ap = ap.bitcast(BF16)
                    ap = ap[:, 0:n]
                if len(shape) == 1:
                    return ap
                if len(shape) == 2:
                    return ap.rearrange("p (a b) -> p a b", a=shape[0], b=shape[1])
                return ap.rearrange("p (a b c) -> p a b c", a=shape[0], b=shape[1], c=shape[2])

        def pvc(name, l, col):
            o = PV_OFF[(name, l)] + col
            return pv[:, o:o + 1]

        def modc(l, i, kt, j):
            return modv[:, l, i * 16 + kt, j:j + 1]

        wstate = {"next_issue": 0, "next_use": 0, "main_idx": 0}
        blk_src = []
        mi = 0
        for b in blocks:
            if b[0] in ("P1c", "P1ns"):
                pi = {"P1c": 0, "P1ns": 1}[b[0]] + 2 * b[1]
                blk_src.append(("pstr", pi))
            else:
                blk_src.append(("ws", mi))
                mi += 1

        XS0 = (SCR_BYTES - 2 * BLK * 2) // 4
        blk_slot, blk_prev = [], []
        last_in_slot = {}
        cnt = {0: 0}
        for i in range(NB):
            ph = bphase[i]
            if ph not in cnt:
                cnt[ph] = 0
            ring = [0, 1, 2, 3] if ph > 0 else [0, 1]
            s_ = ring[cnt[ph] % len(ring)]
            cnt[ph] += 1
            blk_slot.append(s_)
            blk_prev.append(last_in_slot.get(s_, -1))
            last_in_slot[s_] = i

        def slot_ap(slot):
            if slot < 2:
                return wr[:, slot, :]
            o = XS0 + (slot - 2) * (BLK // 2)
            return scr[:, o:o + BLK // 2].bitcast(BF16)

        def issue_block(i):
            slot = blk_slot[i]
            tname, ti = blk_src[i]
            src = dr[tname][ti].rearrange("p (a b) -> p a b", a=8, b=512)
            dst = slot_ap(slot).rearrange("p (a b) -> p a b", a=8, b=512)
            S.dma("pool", lambda e, dst=dst, src=src: e.dma_start(out=dst, in_=src),
                  "w%d" % slot, reads=(), writes=(("w", slot),))

        def wnext(desc):
            i = wstate["next_use"]
            assert blocks[i] == desc, (i, blocks[i], desc)
            while wstate["next_issue"] < NB:
                j = wstate["next_issue"]
                if j > i + 3:
                    break
                if blk_prev[j] >= i:
                    break
                if blk_slot[j] >= 2 and bphase[j] != bphase[i]:
                    break
                issue_block(j)
                wstate["next_issue"] += 1
            assert wstate["next_issue"] > i
            wstate["next_use"] += 1
            slot = blk_slot[i]
            return slot_ap(slot), ("w", slot)

        def mm_group(bank, W, mms, reads, extra_writes=(), first=True, last=True):
            out = ps[:, bank, 0:W]

            def fn(e, out=out, mms=mms):
                ins = None
                n = len(mms)
                for i, (lt, rh) in enumerate(mms):
                    ins = e.matmul(out, lt, rh, start=(first and i == 0), stop=(last and i == n - 1))
                return ins
            S.op("pe", fn, reads=reads, writes=(("ps", bank),) + tuple(extra_writes))

        def act(out, in_, func, reads, writes, scale=1.0, bias=None):
            kw = {}
            if bias is not None:
                kw["bias"] = bias
            S.op("act", lambda e: e.activation(out=out, in_=in_, func=func, scale=scale, **kw),
                 reads=reads, writes=writes)

        def stt(out, in0, scalar, in1, op0, op1, reads, writes):
            S.op("dve", lambda e: e.scalar_tensor_tensor(out=out, in0=in0, scalar=scalar, in1=in1,
                                                         op0=op0, op1=op1),
                 reads=reads, writes=writes)

        def ts(out, in0, s1, s2, op0, op1, reads, writes, eng="dve"):
            if s2 is None:
                S.op(eng, lambda e: e.tensor_scalar(out=out, in0=in0, scalar1=s1, scalar2=None, op0=op0),
                     reads=reads, writes=writes)
            else:
                S.op(eng, lambda e: e.tensor_scalar(out=out, in0=in0, scalar1=s1, scalar2=s2, op0=op0, op1=op1),
                     reads=reads, writes=writes)

        def tt(out, in0, in1, op, reads, writes, eng="dve"):
            S.op(eng, lambda e: e.tensor_tensor(out=out, in0=in0, in1=in1, op=op), reads=reads, writes=writes)

        def recip(out, in_, reads, writes):
            S.op("dve", lambda e: e.reciprocal(out=out, in_=in_), reads=reads, writes=writes)

        S.op("dve", lambda e: e.memset(cst[:, 0:1], EPS), writes=(("cst",),))
        S.op("dve", lambda e: e.memset(cst[:, 1:2], 1.0), writes=(("cst",),))
        S.op("dve", lambda e: e.memset(cst[:, 2:3], 0.0), writes=(("cst",),))
        S.op("dve", lambda e: e.memset(cst[:, 3:4], 0.25), writes=(("cst",),))
        S.op("dve", lambda e: e.memset(ones[:], 1.0), writes=(("ones",),))
        S.op("dve", lambda e: e.memset(scr[:], 0.0), writes=(("scrz",),))
        S.op("dve", lambda e: e.memset(stout[:], 0.0), writes=(("stout",),))
        S.dma("sp", lambda e: e.dma_start(out=pv[:], in_=dr["pv"]), "pv", writes=(("pv",),))
        S.dma("sp", lambda e: e.dma_start(out=cvt[:], in_=dr["cv"]), "cv", writes=(("cvt",),))
        S.dma("sp", lambda e: e.dma_start(out=h0t[:], in_=dr["h0"]), "h0", writes=(("h0t",),))
        S.dma("sp", lambda e: e.dma_start(out=maskt[:], in_=dr["mask"]), "mask", writes=(("maskt",),))
        S.dma("pool", lambda e: e.dma_start(out=dsm[:], in_=dr["dsm"]), "dsm", writes=(("dsm",),))
        for kt in range(16):
            keys = tuple(("x", kt, c) for c in range(3))
            S.dma("sp", lambda e, kt=kt: e.dma_start(out=x[:, kt, :], in_=dr["xT"][:, kt, :]),
                  "x%d" % kt, writes=keys)
        for j_ in range(2):
            issue_block(j_)
        wstate["next_issue"] = 2
        for kt in range(16):
            keys = tuple(("x", kt, c) for c in range(2))
            S.dma("pool", lambda e, kt=kt: e.dma_start(out=x[:, kt, 0:1024], in_=dr["pos"][:, kt, :],
                                                        accum_op=ALU.add),
                  "xp%d" % kt, reads=keys, writes=keys)

        act(cvth[:], cvt[:], AF.Tanh, reads=(("cvt",),), writes=(("cvth",),), scale=0.5)
        stt(cvth[:], cvth[:], 1.0, cvt[:], ALU.add, ALU.mult, reads=(("cvth",), ("cvt",)), writes=(("cvth",),))
        ts(scv[:], cvth[:], 0.5, None, ALU.mult, None, reads=(("cvth",),), writes=(("scv",),))

        def mod_block(l, b):
            blk, wkey = wnext(("mod", l, b))
            bv = blk.rearrange("p (k n) -> p k n", k=16, n=256)
            for tl in range(2):
                ft = 2 * b + tl
                out = ps[:, 7, ft * 2:ft * 2 + 2]

                def fn(e, out=out, bv=bv, tl=tl):
                    ins = None
                    for kt in range(16):
                        ins = e.matmul(out, bv[:, kt, tl * 128:(tl + 1) * 128], scv[:, kt * 2:kt * 2 + 2],
                                       start=(kt == 0), stop=(kt == 15))
                    return ins
                S.op("pe", fn, reads=(wkey, ("scv",)), writes=(("psmod",),))

        def mod_finalize(l, half):
            pm = ps[:, 7, 0:192].rearrange("p (f j) -> p f j", f=96, j=2)
            bo = PV_OFF[("bmod", l)]
            f0, f1 = {0: (0, 32), 2: (32, 48), 1: (48, 80), 3: (80, 96)}[half]
            for j in range(2):
                tt(modv[:, l, f0:f1, j], pm[:, f0:f1, j], pv[:, bo + f0:bo + f1], ALU.add,
                   reads=(("psmod",), ("pv",)), writes=(("modv", l, half),))
            if half in (2, 3):
                return
            which, (gname, si) = half, (("nm", 1), ("nf", 4))[half]
            go = PV_OFF[(gname, l)]
            for j in range(2):
                stt(Amod[:, l, which, :, j], modv[:, l, si * 16:(si + 1) * 16, j], 1.0, pv[:, go:go + 16],
                    ALU.add, ALU.mult, reads=(("modv", l, half), ("pv",)), writes=(("Amod", l, half),))
            if half == 1:
                return
            o = PV_OFF[("ba", l)]
            ts(lrup[:, l, 0, :], pv[:, o:o + 16], 0.5, None, ALU.mult, None, reads=(("pv",),), writes=(("lrup", l),))
            o = PV_OFF[("bx", l)]
            ts(lrup[:, l, 1, :], pv[:, o:o + 16], 0.5, None, ALU.mult, None, reads=(("pv",),), writes=(("lrup", l),))
            o = PV_OFF[("lam", l)]
            act(lrup[:, l, 2, :], pv[:, o:o + 16], AF.Exp, reads=(("pv",),), writes=(("lrup", l),), scale=-1.0)
            act(lrup[:, l, 2, :], lrup[:, l, 2, :], AF.Ln, reads=(("lrup", l),), writes=(("lrup", l),),
                scale=1.0, bias=cst[:, 1:2])
            ts(lrup[:, l, 2, :], lrup[:, l, 2, :], -4.0, None, ALU.mult, None,
               reads=(("lrup", l),), writes=(("lrup", l),))

        def colsum_rstd(src_fn, ntiles, inv_n, src_reads_fn, tag):
            banks = []
            for c, (t0, W) in enumerate(CH):
                bank = S.bank()
                banks.append(bank)
                for i in range(ntiles):
                    sl = i % 2
                    if sl == 0:
                        act(sqb[:, sl, 0:W], src_fn(i, c), AF.Square, reads=src_reads_fn(i, c), writes=(("sqb", sl),))
                    else:
                        tt(sqb[:, sl, 0:W], src_fn(i, c), src_fn(i, c), ALU.mult, reads=src_reads_fn(i, c),
                           writes=(("sqb", sl),))
                    out = ps[:, bank, 0:W]
                    S.op("pe", lambda e, out=out, sl=sl, W=W, i=i: e.matmul(out, ones[:], sqb[:, sl, 0:W],
                                                                          start=(i == 0), stop=(i == ntiles - 1)),
                         reads=(("sqb", sl), ("ones",)), writes=(("ps", bank),))
            for c, (t0, W) in enumerate(CH):
                act(rstd[:, t0:t0 + W], ps[:, banks[c], 0:W], AF.Sqrt, reads=(("ps", banks[c]),), writes=(("rstd", c),),
                    scale=inv_n, bias=cst[:, 0:1])
            for c, (t0, W) in enumerate(CH):
                recip(rstd[:, t0:t0 + W], rstd[:, t0:t0 + W], reads=(("rstd", c),), writes=(("rstd", c),))

        def norm_stats():
            colsum_rstd(lambda kt, c: x[:, kt, CH[c][0]:CH[c][0] + CH[c][1]], 16, 1.0 / D,
                        lambda kt, c: (("x", kt, c),), "nm")

        def norm_mod(l, which, tmpf, stats=True):
            if stats:
                norm_stats()
            si = 0 if which == 0 else 3
            for c, (t0, W) in enumerate(CH):
                j = CJ[c]
                for kt in range(16):
                    sl = kt % 2
                    stt(tmpf[:, sl, 0:W], x[:, kt, t0:t0 + W], Amod[:, l, which, kt, j:j + 1], rstd[:, t0:t0 + W],
                        ALU.mult, ALU.mult, reads=(("x", kt, c), ("Amod", l, which), ("rstd", c)), writes=(("tmpf", l, which, sl),))
                    act(hb[:, kt, t0:t0 + W], tmpf[:, sl, 0:W], AF.Identity,
                        reads=(("tmpf", l, which, sl), ("modv", l, which)), writes=(("hb", kt, c),), scale=1.0, bias=modc(l, si, kt, j))

        def group_finish(l, mG, mkey, ntiles, gcol0, out_name, nblk, kt_per_blk_cols, tag, after_blk=None):
            colsum_rstd(lambda i, c: mG[:, i, CH[c][0]:CH[c][0] + CH[c][1]], ntiles, 1.0 / (ntiles * 128),
                        lambda i, c: ((mkey, i),), tag)
            for i in range(ntiles):
                stt(mG[:, i, :], mG[:, i, :], pvc("on", l, gcol0 + i), rstd[:, :], ALU.mult, ALU.mult,
                    reads=((mkey, i), ("pv",), ("rstd", 0), ("rstd", 1), ("rstd", 2)), writes=((mkey, i),))
            ncols = BLK // ntiles
            nft = ncols // 128
            for q in range(nblk):
                blk, wkey = wnext((out_name, l, q))
                bv = blk.rearrange("p (k n) -> p k n", k=ntiles, n=ncols)
                for fl in range(nft):
                    ft = q * nft + fl
                    for c, (t0, W) in enumerate(CH):
                        bank = S.bank()
                        mm_group(bank, W, [(bv[:, kt, fl * 128:(fl + 1) * 128], mG[:, kt, t0:t0 + W])
                                           for kt in range(ntiles)],
                                 reads=(wkey,) + tuple((mkey, kt) for kt in range(ntiles)))
                        stt(x[:, ft, t0:t0 + W], ps[:, bank, 0:W], modc(l, 2, ft, CJ[c]), x[:, ft, t0:t0 + W],
                            ALU.mult, ALU.add, reads=(("ps", bank), ("modv", l, 2), ("x", ft, c)), writes=(("x", ft, c),))
                if after_blk is not None:
                    after_blk()

        def group_A(l):
            S.barrier()
            cv_ = Carve()
            mA = cv_.take([8, NT], BF16)
            xpad = cv_.take([NSEG, 286], BF16)
            dgA = cv_.take([4, 128], BF16)
            xc = cv_.take([NT], F32)
            xcb = cv_.take([NT], BF16)
            A1 = cv_.take([NT], F32)
            A2 = cv_.take([NT], F32)
            A3 = cv_.take([NT], F32)
            A4 = cv_.take([NT], F32)
            T = "A%d" % l
            ident = dsm[:, 1280:1408]
            mA32 = scr[:, 0:5120]
            RK = (("rstd", 0), ("rstd", 1), ("rstd", 2))
            S.op("dve", lambda e: e.memset(xpad[:, :, 0:2], 0.0), writes=((T, "xpad"),))
            S.op("dve", lambda e: e.memset(xpad[:, :, 258:286], 0.0), writes=((T, "xpad"),))
            def make_tile(n):
                st = {}
                def head():
                    blk, wkey = wnext(("inA", l, n))
                    bv = blk.rearrange("p (k n) -> p k n", k=16, n=256)
                    mod_after = [16 + 2 * n, 17 + 2 * n] if (l == 0 and n < 4) else []
                    lsl = n % 2
                    S.dma("pool", lambda e, lsl=lsl, n=n: e.dma_start(out=lruwb[:, lsl, :], in_=dr["lruw"][l * 8 + n]),
                          "lruw%d" % lsl, writes=(("lruw", lsl),))
                    lw = lruwb[:, lsl, :].rearrange("p (g j) -> p g j", g=4, j=128)
                    S.ring = [0, 1, 2, 3]
                    bya = [4, 5, 6]
                    for k in range(4):
                        act(dgA[:, k, :], ident, AF.Copy, reads=(("dsm",), ("pv",)), writes=((T, "dgA"),),
                            scale=pvc("caw", l, k * 8 + n))
                    bxa = []
                    for c, (t0, W) in enumerate(CH):
                        bank = S.bank()
                        bxa.append(bank)
                        mm_group(bank, W, [(bv[:, kt, 0:128], hb[:, kt, t0:t0 + W]) for kt in range(16)],
                                 reads=(wkey,) + tuple(("hb", kt, c) for kt in range(16)))
                    for c, (t0, W) in enumerate(CH):
                        g0, ng = CHSEG[c]
                        act(xpad[:, g0:g0 + ng, 2:258], ps[:, bxa[c], 0:W].rearrange("p (a b) -> p a b", a=ng, b=SEG),
                            AF.Copy, reads=(("ps", bxa[c]),), writes=((T, "xpad"),))
                    dbg("evac")
                    act(xpad[:, 1:4, 0:2], xpad[:, 0:3, 256:258], AF.Copy, reads=((T, "xpad"), ("maskt",)),
                        writes=((T, "xpad"),), scale=maskt[:, 0:1])
                    act(xpad[:, 0:3, 258:259], xpad[:, 1:4, 2:3], AF.Copy, reads=((T, "xpad"), ("maskt",)),
                        writes=((T, "xpad"),), scale=maskt[:, 0:1])
                    dbg("pads")
                    for c, (t0, W) in enumerate(CH):
                        g0, ng = CHSEG[c]
                        bank = S.bank()
                        mm_group(bank, W, [(dgA[:, k, :], xpad[:, g0:g0 + ng, k:k + 256]) for k in range(4)],
                                 reads=((T, "dgA"), (T, "xpad")))
                        act(xc[:, t0:t0 + W], ps[:, bank, 0:W], AF.Identity, reads=(("ps", bank), ("pv",)),
                            writes=((T, "xc"),), scale=1.0, bias=pvc("cab", l, n))
                        if os.environ.get("GA_VAR") == "noxcb":
                            continue
                        act(xcb[:, t0:t0 + W], ps[:, bank, 0:W], AF.Identity, reads=(("ps", bank), ("pv",)),
                            writes=((T, "xcb"),), scale=1.0, bias=pvc("cab", l, n))

                    def ya_matmuls():
                        for c, (t0, W) in enumerate(CH):
                            mm_group(4 + c, W, [(bv[:, kt, 128:256], hb[:, kt, t0:t0 + W]) for kt in range(16)],
                                     reads=(wkey,) + tuple(("hb", kt, c) for kt in range(16)))
                        for b_ in mod_after:
                            mod_block(0, b_)

                    st.update(dict(bv=bv, wkey=wkey, lsl=lsl, lw=lw, bya=bya, ya_matmuls=ya_matmuls))
                def main():
                    bv, wkey, lsl, lw, bya, ya_matmuls = (st[k] for k in ('bv', 'wkey', 'lsl', 'lw', 'bya', 'ya_matmuls'))
                    bufs = {0: dict(a=A1[:], ka=((T, "A1"),), s=A2[:], ks=((T, "A2"),), u=A3[:], ku=((T, "A3"),))}
                    if n <= 5:
                        u1 = mA32[:, (n + 1) * 640:(n + 3) * 640]
                        ku1 = ((T + "m", n + 1), (T + "m", n + 2))
                    else:
                        u1, ku1 = A3[:], ((T, "A3"),)
                    bufs[1] = dict(a=A4[:], ka=((T, "A4"),), s=rstd[:, :], ks=RK, u=u1, ku=ku1)
                    if n <= 3:
                        G = mA32[:, (n + 3) * 640:(n + 5) * 640]
                        kG = ((T + "m", n + 3), (T + "m", n + 4))
                    else:
                        G, kG = A1[:], ((T, "A1"),)

                    def lru_act(d, part=None):
                        B = bufs[d]
                        pcol = d * 8 + n
                        for c, (t0, W) in enumerate(CH):
                            if part not in (None, "tanh"):
                                break
                            b1 = S.bank()
                            mm_group(b1, W, [(lw[:, d, :], xcb[:, t0:t0 + W])], reads=(("lruw", lsl), (T, "xcb")))
                            b2 = S.bank()
                            mm_group(b2, W, [(lw[:, 2 + d, :], xcb[:, t0:t0 + W])], reads=(("lruw", lsl), (T, "xcb")))
                            act(B["a"][:, t0:t0 + W], ps[:, b1, 0:W], AF.Tanh, reads=(("ps", b1), ("lrup", l)),
                                writes=B["ka"], scale=0.5, bias=lrup[:, l, 0, pcol:pcol + 1])
                            act(B["u"][:, t0:t0 + W], ps[:, b2, 0:W], AF.Tanh, reads=(("ps", b2), ("lrup", l)),
                                writes=B["ku"], scale=0.5, bias=lrup[:, l, 1, pcol:pcol + 1])
                        if part in (None, "exp"):
                            act(B["a"], B["a"], AF.Exp, reads=B["ka"] + (("lrup", l),), writes=B["ka"],
                                scale=lrup[:, l, 2, pcol:pcol + 1], bias=lrup[:, l, 2, pcol:pcol + 1])
                            act(B["s"], B["a"], AF.Square, reads=B["ka"], writes=B["ks"])
                        if part in (None, "sqrt"):
                            act(B["s"], B["s"], AF.Sqrt, reads=B["ks"], writes=B["ks"], scale=-0.25, bias=cst[:, 3:4])

                    def lru_dve(d):
                        B = bufs[d]
                        stt(B["u"], B["u"], 1.0, B["s"], ALU.add, ALU.mult, reads=B["ku"] + B["ks"], writes=B["ku"])
                        tt(B["u"], B["u"], xc[:], ALU.mult, reads=B["ku"] + ((T, "xc"),), writes=B["ku"])
                        if d == 0:
                            bcols = B["a"][:, 256:1024:512]
                        else:
                            bcols = B["a"][:, 255:1023:512]
                        ts(bcols, bcols, maskt[:, 0:1], None, ALU.mult, None, reads=B["ka"] + (("maskt",),), writes=B["ka"])
                        hbase = ((l * 2 + d) * 8 + n) * NSEG
                        Hd = B["s"]
                        if d == 0:
                            plan_ = [(0, 512, 0, None), (512, 1024, 2, 511), (1024, 1280, 4, None)]
                        else:
                            plan_ = [(512, 1024, 3, None), (0, 512, 1, 512), (1024, 1280, 4, None)]
                        for (c0, c1, gi, carry) in plan_:
                            if carry is None:
                                init = h0t[:, hbase + gi:hbase + gi + 1]
                                rkeys = (("h0t",),)
                            else:
                                ib = d * 8 + gi
                                stt(initb[:, ib:ib + 1], Hd[:, carry:carry + 1], maskt[:, 0:1], h0t[:, hbase + gi:hbase + gi + 1],
                                    ALU.mult, ALU.add, reads=B["ks"] + (("maskt",), ("h0t",)), writes=(("initb", ib),))
                                init = initb[:, ib:ib + 1]
                                rkeys = (("initb", ib),)
                            if d == 0:
                                o_, a_, u_ = Hd[:, c0:c1], B["a"][:, c0:c1], B["u"][:, c0:c1]
                            else:
                                o_, a_, u_ = Hd[:, c0:c1][:, ::-1], B["a"][:, c0:c1][:, ::-1], B["u"][:, c0:c1][:, ::-1]
                            S.op("dve", lambda e, o_=o_, a_=a_, u_=u_, init=init: e.tensor_tensor_scan(
                                out=o_, data0=a_, data1=u_, initial=init, op0=ALU.mult, op1=ALU.add),
                                reads=B["ka"] + B["ku"] + rkeys, writes=B["ks"])
                        Hv = Hd.rearrange("p (g s) -> p g s", g=NSEG, s=SEG)
                        col = SEG - 1 if d == 0 else 0
                        S.op("dve", lambda e, Hv=Hv, col=col, hbase=hbase: e.tensor_copy(
                            out=stout[:, hbase:hbase + NSEG], in_=Hv[:, :, col]),
                            reads=B["ks"], writes=(("stout",),))

                    def gelu_sq():
                        for c, (t0, W) in enumerate(CH):
                            act(G[:, t0:t0 + W], ps[:, bya[c], 0:W], AF.Square, reads=(("ps", bya[c]),), writes=kG,
                                scale=0.21145921594497926)

                    def gelu_poly():
                        for c, (t0, W) in enumerate(CH):
                            stt(G[:, t0:t0 + W], G[:, t0:t0 + W], 1.0, ps[:, bya[c], 0:W], ALU.add, ALU.mult,
                                reads=kG + (("ps", bya[c]),), writes=kG)

                    def gelu_tanh():
                        for c, (t0, W) in enumerate(CH):
                            act(G[:, t0:t0 + W], G[:, t0:t0 + W], AF.Tanh, reads=kG, writes=kG, scale=0.7978845608028654)

                    def gelu_fin():
                        for c, (t0, W) in enumerate(CH):
                            yp = ps[:, bya[c], 0:W]
                            stt(G[:, t0:t0 + W], G[:, t0:t0 + W], 1.0, yp, ALU.add, ALU.mult,
                                reads=kG + (("ps", bya[c]),), writes=kG)

                    early_gelu = n <= 3
                    if n <= 5:
                        lru_act(0, "tanh")
                        lru_act(1, "tanh")
                        dbg("tanh")
                        ya_matmuls()
                        lru_act(0, "exp")
                        lru_act(1, "exp")
                        lru_act(0, "sqrt")
                        lru_act(1, "sqrt")
                        dbg("sqrt")
                        if early_gelu:
                            gelu_sq()
                        dbg("gsq")
                        lru_dve(0)
                        dbg("dve0")
                        if early_gelu:
                            gelu_poly()
                            gelu_tanh()
                        dbg("gpoly")
                        lru_dve(1)
                        dbg("dve1")
                    else:
                        lru_act(0)
                        ya_matmuls()
                        lru_dve(0)
                        lru_act(1)
                        lru_dve(1)
                    st.update(dict(gelu_sq=gelu_sq, gelu_poly=gelu_poly, gelu_tanh=gelu_tanh, gelu_fin=gelu_fin, G=G, kG=kG, early_gelu=early_gelu))
                def tail():
                    gelu_sq, gelu_poly, gelu_tanh, gelu_fin, G, kG, early_gelu = (st[k] for k in ('gelu_sq', 'gelu_poly', 'gelu_tanh', 'gelu_fin', 'G', 'kG', 'early_gelu'))
                    if not early_gelu:
                        gelu_sq()
                        gelu_poly()
                        gelu_tanh()
                    gelu_fin()
                    tt(A2[:], A2[:], rstd[:, :], ALU.add, reads=((T, "A2"),) + RK, writes=((T, "A2"),))
                    stt(mA[:, n, :], G, 0.5, A2[:], ALU.mult, ALU.mult, reads=kG + ((T, "A2"),),
                        writes=((T + "m", n),))
                return head, main, tail

            tiles = [make_tile(n) for n in range(8)]
            tiles[0][0]()
            for n in range(8):
                tiles[n][1]()
                if n + 1 < 8:
                    tiles[n + 1][0]()
                tiles[n][2]()
            S.ring = list(range(7))
            if l == 0:
                mod_finalize(l, 2)
            afq = [24]

            def afmods():
                for _ in range(2):
                    mod_block(0, afq[0])
                    afq[0] += 1
            if l == 0:
                S.barrier()
            group_finish(l, mA, T + "m", 8, 0, "outA", 4, None, T, after_blk=(afmods if l == 0 else None))

        def group_B(l):
            S.barrier()
            cv_ = Carve()
            Y = cv_.take([4, 10, 256], BF16)
            mB = cv_.take([4, NT], BF16)
            xbT = cv_.take([2, NT], BF16)
            T = "B%d" % l
            ccsc = dsm[:, 0:256]
            p2c = dsm[:, 256:768].rearrange("p (a b) -> p a b", a=2, b=256)
            p2s = dsm[:, 768:1280].rearrange("p (a b) -> p a b", a=2, b=256)
            bq = [32]

            def bmods():
                if l == 0:
                    mod_block(0, bq[0])
                    bq[0] += 1
            for b in range(2):
                blk, wkey = wnext(("inB", l, b))
                bv = blk.rearrange("p (k n) -> p k n", k=16, n=256)
                for tl in range(2):
                    for c, (t0, W) in enumerate(CH):
                        bank = S.bank()
                        mm_group(bank, W, [(bv[:, kt, tl * 128:(tl + 1) * 128], hb[:, kt, t0:t0 + W]) for kt in range(16)],
                                 reads=(wkey,) + tuple(("hb", kt, c) for kt in range(16)))
                        act(xbT[:, tl, t0:t0 + W], ps[:, bank, 0:W], AF.Copy, reads=(("ps", bank),),
                            writes=((T, "xbT", tl),))
                for tl in range(2):
                    tile_ = 2 * b + tl
                    for st in range(10):
                        bank = S.bank()
                        mm_group(bank, 256, [(xbT[:, tl, st * 128:(st + 1) * 128], ccsc)],
                                 reads=((T, "xbT", tl), ("dsm",)))
                        act(Y[:, tile_, st, :], ps[:, bank, 0:256], AF.Copy, reads=(("ps", bank),),
                            writes=((T, "Y", tile_),))
                bmods()
            for h in range(2):
                bc, kc = wnext(("P1c", h))
                bcv = bc.rearrange("p (k n) -> p k n", k=8, n=512)
                pbanks = [S.bank() for _ in range(4)]
                for tile_ in range(4):
                    mm_group(pbanks[tile_], 512, [(Y[:, tile_, st, 0:128], bcv[:, st, :]) for st in range(8)],
                             reads=(kc, (T, "Y", tile_)), first=True, last=False)
                bmods()
                bs, ks = wnext(("P1ns", h))
                bsv = bs.rearrange("p (k n) -> p k n", k=8, n=512)
                for tile_ in range(4):
                    bank = pbanks[tile_]
                    mm_group(bank, 512, [(Y[:, tile_, st, 128:256], bsv[:, st, :]) for st in range(8)],
                             reads=(ks, (T, "Y", tile_)), first=False, last=True)
                    act(mB[:, tile_, h * 512:(h + 1) * 512], ps[:, bank, 0:512], AF.Copy, reads=(("ps", bank),),
                        writes=((T + "m", tile_),))
                bmods()
            for tile_ in range(4):
                bank = S.bank()
                mms = []
                for st in range(2):
                    mms.append((Y[:, tile_, 8 + st, 0:128], p2c[:, st, :]))
                    mms.append((Y[:, tile_, 8 + st, 128:256], p2s[:, st, :]))
                mm_group(bank, 256, mms, reads=(("dsm",), (T, "Y", tile_)))
                act(mB[:, tile_, 1024:1280], ps[:, bank, 0:256], AF.Copy, reads=(("ps", bank),),
                    writes=((T + "m", tile_),))
            group_finish(l, mB, T + "m", 4, 8, "outB", 2, None, T, after_blk=bmods)

        def group_C(l):
            S.barrier()
            cv_ = Carve()
            vc = cv_.take([4, NT], F32)
            mC = cv_.take([4, NT], BF16)
            vpad = cv_.take([NSEG, 286], BF16)
            T1 = cv_.take([NT], F32)
            mean = cv_.take([NT], F32)
            dg = cv_.take([31, 128], BF16)
            T = "C%d" % l
            ident = dsm[:, 1280:1408]
            S.op("dve", lambda e: e.memset(vpad[:, :, 0:15], 0.0), writes=((T, "vpad"),))
            S.op("dve", lambda e: e.memset(vpad[:, :, 271:286], 0.0), writes=((T, "vpad"),))
            for j in range(4):
                blk, wkey = wnext(("inC", l, j))
                bv = blk.rearrange("p (k n) -> p k n", k=16, n=256)
                for k in range(31):
                    if k % 2 == 0:
                        ts(dg[:, k, :], ident, pvc("ccw", l, k * 4 + j), None, ALU.mult, None,
                           reads=(("dsm",), ("pv",)), writes=((T, "dg"),))
                    else:
                        act(dg[:, k, :], ident, AF.Copy, reads=(("dsm",), ("pv",)), writes=((T, "dg"),),
                            scale=pvc("ccw", l, k * 4 + j))
                for c, (t0, W) in enumerate(CH):
                    g0, ng = CHSEG[c]
                    bvv = S.bank()
                    mm_group(bvv, W, [(bv[:, kt, 0:128], hb[:, kt, t0:t0 + W]) for kt in range(16)],
                             reads=(wkey,) + tuple(("hb", kt, c) for kt in range(16)))
                    bg = S.bank()
                    mm_group(bg, W, [(bv[:, kt, 128:256], hb[:, kt, t0:t0 + W]) for kt in range(16)],
                             reads=(wkey,) + tuple(("hb", kt, c) for kt in range(16)))
                    act(T1[:, t0:t0 + W], ps[:, bg, 0:W], AF.Tanh, reads=(("ps", bg),), writes=((T, "T1"),), scale=0.5)
                    stt(T1[:, t0:t0 + W], T1[:, t0:t0 + W], 1.0, ps[:, bvv, 0:W], ALU.add, ALU.mult,
                        reads=((T, "T1"), ("ps", bvv)), writes=((T, "T1"),))
                    act(vpad[:, g0:g0 + ng, 15:271], T1[:, t0:t0 + W].rearrange("p (a b) -> p a b", a=ng, b=SEG),
                        AF.Copy, reads=((T, "T1"),), writes=((T, "vpad"),), scale=0.5)
                ts(vpad[:, 1:4, 0:15], vpad[:, 0:3, 256:271], maskt[:, 0:1], None, ALU.mult, None,
                   reads=((T, "vpad"), ("maskt",)), writes=((T, "vpad"),))
                ts(vpad[:, 0:3, 271:286], vpad[:, 1:4, 15:30], maskt[:, 0:1], None, ALU.mult, None,
                   reads=((T, "vpad"), ("maskt",)), writes=((T, "vpad"),))
                for c, (t0, W) in enumerate(CH):
                    g0, ng = CHSEG[c]
                    bank = S.bank()
                    mm_group(bank, W, [(dg[:, k, :], vpad[:, g0:g0 + ng, k:k + 256]) for k in range(31)],
                             reads=((T, "dg"), (T, "vpad")))
                    act(vc[:, j, t0:t0 + W], ps[:, bank, 0:W], AF.Identity, reads=(("ps", bank), ("pv",)),
                        writes=((T, "vc", j),), scale=1.0, bias=pvc("ccb", l, j))
            bm, be = [], []
            for c, (t0, W) in enumerate(CH):
                b1 = S.bank()
                b2 = S.bank()
                bm.append(b1)
                be.append(b2)
                for j in range(4):
                    act(sqb[:, 0, 0:W], vc[:, j, t0:t0 + W], AF.Copy, reads=((T, "vc", j),), writes=(("sqb", 0),))
                    S.op("pe", lambda e, b1=b1, W=W, j=j: e.matmul(ps[:, b1, 0:W], ones[:], sqb[:, 0, 0:W],
                                                                 start=(j == 0), stop=(j == 3)),
                         reads=(("sqb", 0), ("ones",)), writes=(("ps", b1),))
                    act(sqb[:, 1, 0:W], vc[:, j, t0:t0 + W], AF.Square, reads=((T, "vc", j),), writes=(("sqb", 1),))
                    S.op("pe", lambda e, b2=b2, W=W, j=j: e.matmul(ps[:, b2, 0:W], ones[:], sqb[:, 1, 0:W],
                                                                 start=(j == 0), stop=(j == 3)),
                         reads=(("sqb", 1), ("ones",)), writes=(("ps", b2),))
            for c, (t0, W) in enumerate(CH):
                ts(mean[:, t0:t0 + W], ps[:, bm[c], 0:W], 1.0 / D_C, None, ALU.mult, None,
                   reads=(("ps", bm[c]),), writes=((T, "mean"),))
                tt(T1[:, t0:t0 + W], mean[:, t0:t0 + W], mean[:, t0:t0 + W], ALU.mult,
                   reads=((T, "mean"),), writes=((T, "T1"),))
                stt(T1[:, t0:t0 + W], ps[:, be[c], 0:W], 1.0 / D_C, T1[:, t0:t0 + W], ALU.mult, ALU.subtract,
                    reads=(("ps", be[c]), (T, "T1")), writes=((T, "T1"),))
            act(T1[:], T1[:], AF.Sqrt, reads=((T, "T1"),), writes=((T, "T1"),), scale=1.0, bias=cst[:, 0:1])
            recip(rstd[:, :], T1[:], reads=((T, "T1"),), writes=(("rstd", 0), ("rstd", 1), ("rstd", 2)))
            rk = (("rstd", 0), ("rstd", 1), ("rstd", 2))
            for j in range(4):
                tt(vc[:, j, :], vc[:, j, :], mean[:], ALU.subtract, reads=((T, "vc", j), (T, "mean")), writes=((T, "vc", j),))
                tt(vc[:, j, :], vc[:, j, :], rstd[:, :], ALU.mult, reads=((T, "vc", j),) + rk, writes=((T, "vc", j),))
                act(mC[:, j, :], vc[:, j, :], AF.Silu, reads=((T, "vc", j), ("pv",)), writes=((T + "m", j),),
                    scale=pvc("lng", l, j), bias=pvc("lnb", l, j))
            group_finish(l, mC, T + "m", 4, 12, "outC", 2, None, T)

        def ffn(l):
            S.barrier()
            cv_ = Carve()
            actb = cv_.take([8, NT], BF16)
            th = cv_.take([2, 512], F32)
            tmpf = cv_.take([2, 512], F32)
            T = "F%d" % l
            norm_mod(l, 1, tmpf)
            for q in range(6):
                nk = 8 if q < 5 else 4
                for tl in range(nk):
                    t = q * 8 + tl
                    blk, wkey = wnext(("gu", l, t))
                    bv = blk.rearrange("p (k n) -> p k n", k=16, n=256)
                    for c, (t0, W) in enumerate(CH):
                        bg = S.bank()
                        mm_group(bg, W, [(bv[:, kt, 0:128], hb[:, kt, t0:t0 + W]) for kt in range(16)],
                                 reads=(wkey,) + tuple(("hb", kt, c) for kt in range(16)))
                        bu = S.bank()
                        mm_group(bu, W, [(bv[:, kt, 128:256], hb[:, kt, t0:t0 + W]) for kt in range(16)],
                                 reads=(wkey,) + tuple(("hb", kt, c) for kt in range(16)))
                        sl = c % 2
                        act(th[:, sl, 0:W], ps[:, bg, 0:W], AF.Tanh, reads=(("ps", bg),), writes=((T, "th", sl),), scale=0.5)
                        stt(th[:, sl, 0:W], th[:, sl, 0:W], 1.0, ps[:, bg, 0:W], ALU.add, ALU.mult,
                            reads=((T, "th", sl), ("ps", bg)), writes=((T, "th", sl),))
                        stt(actb[:, tl, t0:t0 + W], th[:, sl, 0:W], 0.5, ps[:, bu, 0:W], ALU.mult, ALU.mult,
                            reads=((T, "th", sl), ("ps", bu)), writes=((T, "act", tl, c),))
                    if l == 0:
                        if t < 8:
                            mod_block(0, 40 + t)
                            if t == 7:
                                mod_finalize(0, 3)
                        mod_block(1, t)
                        if t == NHT - 1:
                            for b in range(NHT, 48):
                                mod_block(1, b)
                for r in range(4):
                    blk, wkey = wnext(("down", l, q, r))
                    bv = blk.rearrange("p (k n) -> p k n", k=8, n=512)
                    for fl in range(4):
                        ft = r * 4 + fl
                        for c, (t0, W) in enumerate(CH):
                            bank = S.bank()
                            mm_group(bank, W, [(bv[:, kt, fl * 128:(fl + 1) * 128], actb[:, kt, t0:t0 + W])
                                               for kt in range(nk)],
                                     reads=(wkey,) + tuple((T, "act", kt, c) for kt in range(nk)))
                            stt(x[:, ft, t0:t0 + W], ps[:, bank, 0:W], modc(l, 5, ft, CJ[c]), x[:, ft, t0:t0 + W],
                                ALU.mult, ALU.add, reads=(("ps", bank), ("modv", l, 3), ("x", ft, c)), writes=(("x", ft, c),))

        def finish(final_norm=True):
            S.barrier()
            if final_norm:
                colsum_rstd(lambda kt, c: x[:, kt, CH[c][0]:CH[c][0] + CH[c][1]], 16, 1.0 / D,
                            lambda kt, c: (("x", kt, c),), "fin")
            fo = PV_OFF[("fn", 0)]
            for kt in range(16):
                if final_norm:
                    for c, (t0, W) in enumerate(CH):
                        stt(x[:, kt, t0:t0 + W], x[:, kt, t0:t0 + W], pv[:, fo + kt:fo + kt + 1], rstd[:, t0:t0 + W],
                            ALU.mult, ALU.mult, reads=(("x", kt, c), ("pv",), ("rstd", c)), writes=(("x", kt, c),))
                S.dma("sp", lambda e, kt=kt: e.dma_start(out=dr["yT"][:, kt, :], in_=x[:, kt, :]),
                      "out", reads=tuple(("x", kt, c) for c in range(3)), writes=(("yout", kt),))
            S.dma("sp", lambda e: e.dma_start(out=dr["st"], in_=stout[:]), "out", reads=(("stout",),),
                  writes=(("stdone",),))
            S.emit([(k, v) for k, v in S.dmacount.items()])

        def staged():
            if stop == "load":
                return False
            norm_stats()
            for b in range(16):
                mod_block(0, b)
            for l in range(L):
                mod_finalize(l, 0)
                if l == 1:
                    mod_finalize(l, 2)
                    mod_finalize(l, 1)
                    mod_finalize(l, 3)
                S.barrier()
                norm_mod(l, 0, Carve().take([2, 512], F32), stats=(l > 0))
                if stop == "nm%d" % l:
                    for kt in range(16):
                        for c, (t0, W) in enumerate(CH):
                            act(x[:, kt, t0:t0 + W], hb[:, kt, t0:t0 + W], AF.Copy, reads=(("hb", kt, c),),
                                writes=(("x", kt, c),))
                    return False
                try:
                    group_A(l)
                except StopBuild:
                    return False
                if stop == "A%d" % l:
                    return False
                group_B(l)
                if stop == "B%d" % l:
                    return False
                group_C(l)
                if stop == "C%d" % l:
                    return False
                if l == 0:
                    mod_finalize(l, 1)
                ffn(l)
                if stop == "F%d" % l:
                    return False
            assert wstate["next_use"] == NB, (wstate, NB)
            return True

        finish(final_norm=staged())
    return nc


def _fm(v, ntile):
    return np.ascontiguousarray(np.asarray(v, np.float32).reshape(ntile, 128).T)


def _pos_embed():
    t = np.arange(1024)
    r = (t // 64).astype(np.float32)
    col = (t % 64).astype(np.float32)
    nf = D // 4
    omega = (1.0 / (np.float32(10000.0) ** (np.arange(nf, dtype=np.float32) / np.float32(nf)))).astype(np.float32)

    def enc(p):
        ang = (p[:, None] * omega[None, :]).astype(np.float32)
        return np.concatenate([np.sin(ang), np.cos(ang)], axis=-1)
    return np.concatenate([enc(r), enc(col)], axis=-1).astype(np.float32)


def _kblock(w, kt):
    ncols = w.shape[1]
    return w.reshape(kt, 128, ncols).transpose(1, 0, 2).reshape(128, kt * ncols)


def _dft_mats():
    def cs(n):
        k = np.arange(n, dtype=np.float64)
        ang = 2.0 * np.pi * np.outer(k, k) / n
        return np.cos(ang), np.sin(ang)
    c128, s128 = cs(128)
    c1024, s1024 = cs(1024)
    c256, s256 = cs(256)
    sc1 = 1.0 / math.sqrt(1024 * 128)
    sc2 = 1.0 / math.sqrt(256 * 128)
    full_c = (c1024 * sc1).astype(np.float32)
    full_ns = (-s1024 * sc1).astype(np.float32)
    blk_c = np.zeros((1024, 1024), np.float32)
    blk_ns = np.zeros((1024, 1024), np.float32)
    for g in range(4):
        blk_c[g * 256:(g + 1) * 256, g * 256:(g + 1) * 256] = (c256 * sc2)
        blk_ns[g * 256:(g + 1) * 256, g * 256:(g + 1) * 256] = (-s256 * sc2)

    def pblocks(pc, pns):
        out = np.zeros((4, 128, BLK), np.float32)
        for h in range(2):
            out[2 * h + 0] = _kblock(pc[:, h * 512:(h + 1) * 512], 8)
            out[2 * h + 1] = _kblock(pns[:, h * 512:(h + 1) * 512], 8)
        return out
    dsm = np.zeros((128, 1408), np.float32)
    dsm[:, 1280:1408] = np.eye(128, dtype=np.float32)
    dsm[:, 0:128] = c128
    dsm[:, 128:256] = s128
    dsm[:, 256:768] = _kblock((c256 * sc2).astype(np.float32), 2)
    dsm[:, 768:1280] = _kblock((-s256 * sc2).astype(np.float32), 2)
    return pblocks(full_c, full_ns), pblocks(blk_c, blk_ns), dsm


_CACHE = {}


def prepare_inputs(x_prompt, x_sample, c, state_lru, c_ctx, w_mod, b_mod, norm_mix, norm_ffn, w_in,
                   conv_a_w, conv_a_b, lru_wa, lru_ba, lru_wx, lru_bx, lru_lam, conv_c_w, conv_c_b,
                   ln_c_g, ln_c_b, out_norm, w_out, w_gu, w_down, final_norm):
    f32 = np.float32
    x_prompt = np.asarray(x_prompt, f32)
    x_sample = np.asarray(x_sample, f32)
    c = np.asarray(c, f32)
    state_lru = np.asarray(state_lru, f32)
    c_ctx = np.asarray(c_ctx, f32)
    w_mod = np.asarray(w_mod, f32); b_mod = np.asarray(b_mod, f32)
    w_in = np.asarray(w_in, f32); w_out = np.asarray(w_out, f32)
    w_gu = np.asarray(w_gu, f32); w_down = np.asarray(w_down, f32)
    lru_wa = np.asarray(lru_wa, f32); lru_wx = np.asarray(lru_wx, f32)

    pv = np.zeros((128, NPV), f32)

    def put(name, l, arr):
        o = PV_OFF[(name, l)]
        pv[:, o:o + arr.shape[1]] = arr
    for l in range(L):
        put("nm", l, _fm(norm_mix[l], 16))
        put("nf", l, _fm(norm_ffn[l], 16))
        caw = np.asarray(conv_a_w[l], f32)
        put("caw", l, np.concatenate([_fm(caw[k], 8) for k in range(4)], axis=1))
        put("cab", l, _fm(conv_a_b[l], 8))
        put("ba", l, np.concatenate([_fm(np.asarray(lru_ba)[l, d], 8) for d in range(2)], axis=1))
        put("bx", l, np.concatenate([_fm(np.asarray(lru_bx)[l, d], 8) for d in range(2)], axis=1))
        put("lam", l, np.concatenate([_fm(np.asarray(lru_lam)[l, d], 8) for d in range(2)], axis=1))
        ccw = np.asarray(conv_c_w[l], f32)
        put("ccw", l, np.concatenate([_fm(ccw[k], 4) for k in range(31)], axis=1))
        put("ccb", l, _fm(conv_c_b[l], 4))
        put("lng", l, _fm(ln_c_g[l], 4))
        put("lnb", l, _fm(ln_c_b[l], 4))
        put("on", l, _fm(out_norm[l], 16))
        put("bmod", l, _fm(b_mod[l], 96))
    put("fn", 0, _fm(final_norm, 16))

    blocks, _ = plan_phased()
    main = [b for b in blocks if b[0] not in ("P1c", "P1ns")]
    ws = np.zeros((len(main), 128, BLK), f32)
    for i, b in enumerate(main):
        kind = b[0]
        if kind == "mod":
            _, l, bb = b
            ws[i] = _kblock(w_mod[l][:, bb * 256:(bb + 1) * 256], 16)
        elif kind == "inA":
            _, l, n = b
            w = np.concatenate([w_in[l][:, n * 128:(n + 1) * 128], w_in[l][:, 1024 + n * 128:1024 + (n + 1) * 128]], axis=1)
            ws[i] = _kblock(w, 16)
        elif kind == "inB":
            _, l, bb = b
            ws[i] = _kblock(w_in[l][:, 2048 + bb * 256:2048 + (bb + 1) * 256], 16)
        elif kind == "inC":
            _, l, j = b
            w = np.concatenate([w_in[l][:, 2560 + j * 128:2560 + (j + 1) * 128],
                                w_in[l][:, 3072 + j * 128:3072 + (j + 1) * 128]], axis=1)
            ws[i] = _kblock(w, 16)
        elif kind == "outA":
            _, l, q = b
            ws[i] = _kblock(w_out[l][0:1024, q * 512:(q + 1) * 512], 8)
        elif kind == "outB":
            _, l, q = b
            ws[i] = _kblock(w_out[l][1024:1536, q * 1024:(q + 1) * 1024], 4)
        elif kind == "outC":
            _, l, q = b
            ws[i] = _kblock(w_out[l][1536:2048, q * 1024:(q + 1) * 1024], 4)
        elif kind == "gu":
            _, l, t = b
            w = np.concatenate([w_gu[l][:, t * 128:(t + 1) * 128], w_gu[l][:, D_FF + t * 128:D_FF + (t + 1) * 128]], axis=1)
            ws[i] = _kblock(w, 16)
        elif kind == "down":
            _, l, q, r = b
            nk = 8 if q < 5 else 4
            ws[i][:, :nk * 512] = _kblock(w_down[l][q * 1024:q * 1024 + nk * 128, r * 512:(r + 1) * 512], nk)
        else:
            raise AssertionError(kind)
    lruw = np.zeros((L * 8, 128, 512), f32)
    for l in range(L):
        for n in range(8):
            lruw[l * 8 + n] = np.concatenate([lru_wa[l, 0, n], lru_wa[l, 1, n], lru_wx[l, 0, n], lru_wx[l, 1, n]], axis=1)
    p_full, p_blk, dsm = _dft_mats()
    pos_fm = np.ascontiguousarray(_pos_embed().reshape(1024, 16, 128).transpose(2, 1, 0))
    pos_zero = np.zeros_like(pos_fm)

    in_maps = []
    for core in range(8):
        if core < 2:
            X = np.concatenate([x_sample[core], x_prompt[core]], axis=0)
            cv0 = c[core]
        else:
            X = np.concatenate([x_prompt[2 + 5 * (core - 2) + i] for i in range(5)], axis=0)
            cv0 = c_ctx
        xT = np.ascontiguousarray(X.reshape(NT, 16, 128).transpose(2, 1, 0))
        cv = np.stack([_fm(cv0, 16), _fm(c_ctx, 16)], axis=-1).reshape(128, 32)
        h0 = np.zeros((128, L, 2, 8, NSEG), f32)
        if core < 2:
            for l in range(L):
                h0[:, l, 0, :, 0] = _fm(state_lru[core, l, 0], 8)
                h0[:, l, 1, :, 3] = _fm(state_lru[core, l, 1], 8)
        mask = np.full((128, 1), 1.0 if core < 2 else 0.0, f32)
        in_maps.append({
            "xT": xT, "pos": pos_fm if core < 2 else pos_zero, "cv": np.ascontiguousarray(cv),
            "h0": np.ascontiguousarray(h0.reshape(128, -1)), "mask": mask, "pv": pv, "ws": ws,
            "pstr": p_full if core < 2 else p_blk, "dsm": dsm, "lruw": lruw,
        })

    return in_maps


def kernel(**inputs):
    f32 = np.float32
    in_maps = prepare_inputs(**inputs)
    if "nc" not in _CACHE:
        _CACHE["nc"] = build_program()
    nc = _CACHE["nc"]
    res = run_bass_kernel_spmd(nc, in_maps, core_ids=list(range(8)))
    return assemble(res.results)


def assemble(results):
    f32 = np.float32
    y_prompt = np.zeros((32, 256, D), f32)
    y_sample = np.zeros((2, 1024, D), f32)
    new_state = np.zeros((32, L, 2, D_A), f32)
    for core in range(8):
        r = results[core]
        Y = np.asarray(r["yT"], f32).transpose(2, 1, 0).reshape(NT, D)
        st = np.asarray(r["st"], f32).reshape(128, L, 2, 8, NSEG)
        stg = st.transpose(4, 1, 2, 3, 0).reshape(NSEG, L, 2, D_A)
        if core < 2:
            y_sample[core] = Y[0:1024]
            y_prompt[core] = Y[1024:1280]
            new_state[core] = stg[4]
        else:
            for i in range(5):
                bidx = 2 + 5 * (core - 2) + i
                y_prompt[bidx] = Y[i * 256:(i + 1) * 256]
                new_state[bidx] = stg[i]
    return (y_prompt, y_sample, new_state)
```

```python
import math
import numpy as np
import concourse.bass as bass
import concourse.mybir as mybir
from concourse.bass_utils import run_bass_kernel_spmd

F32 = mybir.dt.float32
BF16 = mybir.dt.bfloat16
AF = mybir.ActivationFunctionType
ALU = mybir.AluOpType

D = 2048
NT = 1280
SEG = 256
NSEG = 5
L = 2
D_A, D_B, D_C = 1024, 512, 512
D_IN = 3584
D_FF = 5632
NHT = D_FF // 128
EPS = 1e-6
CH = [(0, 512), (512, 512), (1024, 256)]
CHSEG = [(0, 2), (2, 2), (4, 1)]
CJ = [0, 0, 1]
BLK = 4096
NSLOT = 2
SAME_ENGINE_SYNC = True

PV_LAYER = [("nm", 16), ("nf", 16), ("caw", 32), ("cab", 8), ("ba", 16), ("bx", 16), ("lam", 16),
            ("ccw", 124), ("ccb", 4), ("lng", 4), ("lnb", 4), ("on", 16), ("bmod", 96)]
PV_OFF = {}
_o = 0
for _l in range(L):
    for _n, _w in PV_LAYER:
        PV_OFF[(_n, _l)] = _o
        _o += _w
PV_OFF[("fn", 0)] = _o
_o += 16
NPV = _o


def plan():
    blocks = []
    for b in range(16):
        blocks.append(("mod", 0, b))
    for l in range(L):
        for n in range(8):
            blocks.append(("inA", l, n))
            if l == 0 and n < 4:
                blocks.append(("mod", 0, 16 + 2 * n))
                blocks.append(("mod", 0, 17 + 2 * n))
        bq = [24]
        if l == 0:
            blocks.append(("b_begin", l))
        for q in range(4):
            blocks.append(("outA", l, q))
            if l == 0:
                for _ in range(2):
                    blocks.append(("mod", 0, bq[0]))
                    bq[0] += 1
        if l == 0:
            blocks.append(("b_end", l))

        def bmods():
            if l == 0:
                blocks.append(("mod", 0, bq[0]))
                bq[0] += 1
        if l == 0:
            blocks.append(("b_begin", l))
        for b in range(2):
            blocks.append(("inB", l, b))
            bmods()
        for h in range(2):
            blocks.append(("P1c", h))
            bmods()
            blocks.append(("P1ns", h))
            bmods()
        for q in range(2):
            blocks.append(("outB", l, q))
            bmods()
        if l == 0:
            blocks.append(("b_end", l))
        for j in range(4):
            blocks.append(("inC", l, j))
        for q in range(2):
            blocks.append(("outC", l, q))
        blocks.append(("ffn_begin", l))
        for q in range(6):
            nk = 8 if q < 5 else 4
            for tl in range(nk):
                t = q * 8 + tl
                blocks.append(("gu", l, t))
                if l == 0:
                    if t < 8:
                        blocks.append(("mod", 0, 40 + t))
                    blocks.append(("mod", 1, t))
                    if t == NHT - 1:
                        for b in range(NHT, 48):
                            blocks.append(("mod", 1, b))
            for r in range(4):
                blocks.append(("down", l, q, r))
        blocks.append(("ffn_end", l))
    return blocks


def plan_phased():
    out, ph = [], []
    cur = 0
    for b in plan():
        if b[0] == "ffn_begin":
            cur = b[1] + 1
        elif b[0] == "b_begin":
            cur = 3
        elif b[0] in ("ffn_end", "b_end"):
            cur = 0
        else:
            out.append(b)
            ph.append(cur)
    return out, ph


class Sched:
    ENG = ("pe", "act", "dve", "pool", "sp")

    def __init__(self, nc):
        self.nc = nc
        self.prog = {e: [] for e in self.ENG}
        self.count = {e: 0 for e in self.ENG}
        self.seen = {e: {} for e in self.ENG}
        self.lastw = {}
        self.readers = {}
        self.dmacount = {}
        self.semnames = list(self.ENG)
        self.nbank = 0
        self.ring = list(range(7))
        self.phase_id = 0
        self.phase_tokens = []
        self.phase_seen = {}

    def _need(self, e, toks):
        need = {}
        for tok in toks:
            if tok is None:
                continue
            s, v = tok
            if s == e and (e in ("pe", "sp") or not SAME_ENGINE_SYNC):
                continue
            if self.seen[e].get(s, 0) >= v:
                continue
            if need.get(s, 0) < v:
                need[s] = v
        for s, v in need.items():
            self.seen[e][s] = v
            self.prog[e].append(("wait", s, v))

    @staticmethod
    def _is_scratch(key):
        k0 = key[0]
        if not isinstance(k0, str):
            return False
        if k0 == "tmpf":
            return True
        if k0 == "w":
            return len(key) > 1 and key[1] in (2, 3)
        return len(k0) >= 2 and k0[0] in "ABCF" and k0[1].isdigit()

    def _deps(self, e, reads, writes):
        toks = []
        for r in reads:
            toks.append(self.lastw.get(r))
        for w in writes:
            if self._is_scratch(w) and self.phase_seen.get(w) != self.phase_id:
                self.phase_seen[w] = self.phase_id
                toks.extend(self.phase_tokens)
            toks.append(self.lastw.get(w))
            for t in self.readers.get(w, ()):
                if t[0] != e:
                    toks.append(t)
        self._need(e, toks)

    def _commit(self, tok, reads, writes):
        for r in reads:
            self.readers.setdefault(r, []).append(tok)
        for w in writes:
            self.lastw[w] = tok
            self.readers[w] = []

    def op(self, e, fn, reads=(), writes=()):
        self._deps(e, reads, writes)
        self.count[e] += 1
        tok = (e, self.count[e])
        self.prog[e].append(("inst", fn, e, 1))
        self._commit(tok, reads, writes)
        return tok

    def dma(self, q, fn, semkey, reads=(), writes=()):
        sname = "d_" + semkey
        if sname not in self.dmacount:
            self.dmacount[sname] = 0
            self.semnames.append(sname)
        self._deps(q, reads, writes)
        self.dmacount[sname] += 16
        tok = (sname, self.dmacount[sname])
        self.prog[q].append(("inst", fn, sname, 16))
        self._commit(tok, reads, writes)
        return tok

    def barrier(self):
        engs = ("pe", "act", "dve", "pool")
        self.phase_id += 1
        self.phase_tokens = [(f, self.count[f]) for f in engs if self.count[f] > 0]
        self.phase_tokens += [(k, v) for k, v in self.dmacount.items() if k.startswith("d_w")]

    def hard_barrier(self):
        engs = ("pe", "act", "dve", "pool")
        for e in engs:
            self._need(e, [(f, self.count[f]) for f in engs if f != e and self.count[f] > 0])

    def bank(self):
        b = self.ring[self.nbank % len(self.ring)]
        self.nbank += 1
        return b

    def emit(self, final_waits):
        nc = self.nc
        engobj = {"pe": "tensor", "act": "scalar", "dve": "vector", "pool": "gpsimd", "sp": "sync"}
        from contextlib import ExitStack
        with ExitStack() as es:
            sems = {}
            for s in self.semnames:
                sems[s] = es.enter_context(nc.semaphore("s_" + s))
            block = es.enter_context(nc.Block())

            def replay(eng, items, extra=()):
                for it in items:
                    if it[0] == "wait":
                        eng.wait_ge(sems[it[1]], it[2])
                    else:
                        ins = it[1](eng)
                        ins.then_inc(sems[it[2]], it[3])
                for s, v in extra:
                    eng.wait_ge(sems[s], v)

            @block.tensor
            def _(eng):
                replay(eng, self.prog["pe"])

            @block.scalar
            def _(eng):
                replay(eng, self.prog["act"])

            @block.vector
            def _(eng):
                replay(eng, self.prog["dve"])

            @block.gpsimd
            def _(eng):
                replay(eng, self.prog["pool"])

            @block.sync
            def _(eng):
                replay(eng, self.prog["sp"], final_waits)


class StopBuild(Exception):
    pass


def build_program(stop=None):
    import os
    ga_stop = os.environ.get("GA_STOP")

    def dbg(step):
        if ga_stop is not None and ga_stop == step:
            raise StopBuild()

    nc = bass.Bass("TRN2", target_bir_lowering=False)
    blocks, bphase = plan_phased()
    NB = len(blocks)
    nmain = sum(1 for b in blocks if b[0] not in ("P1c", "P1ns"))

    dr = {}
    dr["xT"] = nc.dram_tensor("xT", [128, 16, NT], F32, kind="ExternalInput").ap()
    dr["pos"] = nc.dram_tensor("pos", [128, 16, 1024], F32, kind="ExternalInput").ap()
    dr["cv"] = nc.dram_tensor("cv", [128, 32], F32, kind="ExternalInput").ap()
    dr["h0"] = nc.dram_tensor("h0", [128, L * 2 * 8 * NSEG], F32, kind="ExternalInput").ap()
    dr["mask"] = nc.dram_tensor("mask", [128, 1], F32, kind="ExternalInput").ap()
    dr["pv"] = nc.dram_tensor("pv", [128, NPV], F32, kind="ExternalInput").ap()
    dr["ws"] = nc.dram_tensor("ws", [nmain, 128, BLK], F32, kind="ExternalInput").ap()
    dr["pstr"] = nc.dram_tensor("pstr", [4, 128, BLK], F32, kind="ExternalInput").ap()
    dr["dsm"] = nc.dram_tensor("dsm", [128, 1408], F32, kind="ExternalInput").ap()
    dr["lruw"] = nc.dram_tensor("lruw", [L * 8, 128, 512], F32, kind="ExternalInput").ap()
    dr["yT"] = nc.dram_tensor("yT", [128, 16, NT], F32, kind="ExternalOutput").ap()
    dr["st"] = nc.dram_tensor("st", [128, L * 2 * 8 * NSEG], F32, kind="ExternalOutput").ap()

    from contextlib import ExitStack
    with ExitStack() as es:
        def sb(name, shape, dt):
            return es.enter_context(nc.sbuf_tensor("sb_" + name, shape, dt))

        x = sb("x", [128, 16, NT], F32)
        hb = sb("hb", [128, 16, NT], BF16)
        wr = sb("wr", [128, NSLOT, BLK], BF16)
        SCR_BYTES = 53824
        scr = sb("scr", [128, SCR_BYTES // 4], F32)
        pv = sb("pv", [128, NPV], F32)
        modv = sb("modv", [128, L, 96, 2], F32)
        Amod = sb("Amod", [128, L, 2, 16, 2], F32)
        lrup = sb("lrup", [128, L, 3, 16], F32)
        cvt = sb("cvt", [128, 32], F32)
        cvth = sb("cvth", [128, 32], F32)
        scv = sb("scv", [128, 32], BF16)
        h0t = sb("h0t", [128, L * 2 * 8 * NSEG], F32)
        stout = sb("stout", [128, L * 2 * 8 * NSEG], F32)
        maskt = sb("maskt", [128, 1], F32)
        cst = sb("cst", [128, 4], F32)
        ones = sb("ones", [128, 128], BF16)
        dsm = sb("dsm", [128, 1408], BF16)
        lruwb = sb("lruwb", [128, 2, 512], BF16)
        sqb = sb("sqb", [128, 2, 512], BF16)
        rstd = sb("rstd", [128, NT], F32)
        initb = sb("initb", [128, 16], F32)
        assert True
        ps = es.enter_context(nc.psum_tensor("ps", [128, 8, 512], F32))

        S = Sched(nc)

        class Carve:
            def __init__(self):
                self.off = 0

            def take(self, shape, dt):
                n = int(np.prod(shape))
                nbytes = n * (4 if dt == F32 else 2)
                nwords = (nbytes + 3) // 4
                assert (self.off + nwords) * 4 <= SCR_BYTES, (self.off, nwords)
                ap = scr[:, self.off:self.off + nwords]
                self.off += nwords
                if dt == BF16:
                    ap = ap.bitcast(BF16)
                    ap = ap[:, 0:n]
                if len(shape) == 1:
                    return ap
                if len(shape) == 2:
                    return ap.rearrange("p (a b) -> p a b", a=shape[0], b=shape[1])
                return ap.rearrange("p (a b c) -> p a b c", a=shape[0], b=shape[1], c=shape[2])

        def pvc(name, l, col):
            o = PV_OFF[(name, l)] + col
            return pv[:, o:o + 1]

        def modc(l, i, kt, j):
            return modv[:, l, i * 16 + kt, j:j + 1]

        wstate = {"next_issue": 0, "next_use": 0, "main_idx": 0}
        blk_src = []
        mi = 0
        for b in blocks:
            if b[0] in ("P1c", "P1ns"):
                pi = {"P1c": 0, "P1ns": 1}[b[0]] + 2 * b[1]
                blk_src.append(("pstr", pi))
            else:
                blk_src.append(("ws", mi))
                mi += 1

        XS0 = (SCR_BYTES - 2 * BLK * 2) // 4
        blk_slot, blk_prev = [], []
        last_in_slot = {}
        cnt = {0: 0}
        for i in range(NB):
            ph = bphase[i]
            if ph not in cnt:
                cnt[ph] = 0
            ring = [0, 1, 2, 3] if ph > 0 else [0, 1]
            s_ = ring[cnt[ph] % len(ring)]
            cnt[ph] += 1
            blk_slot.append(s_)
            blk_prev.append(last_in_slot.get(s_, -1))
            last_in_slot[s_] = i

        def slot_ap(slot):
            if slot < 2:
                return wr[:, slot, :]
            o = XS0 + (slot - 2) * (BLK // 2)
            return scr[:, o:o + BLK // 2].bitcast(BF16)

        def issue_block(i):
            slot = blk_slot[i]
            tname, ti = blk_src[i]
            src = dr[tname][ti].rearrange("p (a b) -> p a b", a=8, b=512)
            dst = slot_ap(slot).rearrange("p (a b) -> p a b", a=8, b=512)
            S.dma("pool", lambda e, dst=dst, src=src: e.dma_start(out=dst, in_=src),
                  "w%d" % slot, reads=(), writes=(("w", slot),))

        def wnext(desc):
            i = wstate["next_use"]
            assert blocks[i] == desc, (i, blocks[i], desc)
            while wstate["next_issue"] < NB:
                j = wstate["next_issue"]
                if j > i + 3:
                    break
                if blk_prev[j] >= i:
                    break
                if blk_slot[j] >= 2 and bphase[j] != bphase[i]:
                    break
                issue_block(j)
                wstate["next_issue"] += 1
            assert wstate["next_issue"] > i
            wstate["next_use"] += 1
            slot = blk_slot[i]
            return slot_ap(slot), ("w", slot)

        def mm_group(bank, W, mms, reads, extra_writes=(), first=True, last=True):
            out = ps[:, bank, 0:W]

            def fn(e, out=out, mms=mms):
                ins = None
                n = len(mms)
                for i, (lt, rh) in enumerate(mms):
                    ins = e.matmul(out, lt, rh, start=(first and i == 0), stop=(last and i == n - 1))
                return ins
            S.op("pe", fn, reads=reads, writes=(("ps", bank),) + tuple(extra_writes))

        def act(out, in_, func, reads, writes, scale=1.0, bias=None):
            kw = {}
            if bias is not None:
                kw["bias"] = bias
            S.op("act", lambda e: e.activation(out=out, in_=in_, func=func, scale=scale, **kw),
                 reads=reads, writes=writes)

        def stt(out, in0, scalar, in1, op0, op1, reads, writes):
            S.op("dve", lambda e: e.scalar_tensor_tensor(out=out, in0=in0, scalar=scalar, in1=in1,
                                                         op0=op0, op1=op1),
                 reads=reads, writes=writes)

        def ts(out, in0, s1, s2, op0, op1, reads, writes, eng="dve"):
            if s2 is None:
                S.op(eng, lambda e: e.tensor_scalar(out=out, in0=in0, scalar1=s1, scalar2=None, op0=op0),
                     reads=reads, writes=writes)
            else:
                S.op(eng, lambda e: e.tensor_scalar(out=out, in0=in0, scalar1=s1, scalar2=s2, op0=op0, op1=op1),
                     reads=reads, writes=writes)

        def tt(out, in0, in1, op, reads, writes, eng="dve"):
            S.op(eng, lambda e: e.tensor_tensor(out=out, in0=in0, in1=in1, op=op), reads=reads, writes=writes)

        def recip(out, in_, reads, writes):
            S.op("dve", lambda e: e.reciprocal(out=out, in_=in_), reads=reads, writes=writes)

        S.op("dve", lambda e: e.memset(cst[:, 0:1], EPS), writes=(("cst",),))
        S.op("dve", lambda e: e.memset(cst[:, 1:2], 1.0), writes=(("cst",),))
        S.op("dve", lambda e: e.memset(cst[:, 2:3], 0.0), writes=(("cst",),))
        S.op("dve", lambda e: e.memset(cst[:, 3:4], 0.25), writes=(("cst",),))
        S.op("dve", lambda e: e.memset(ones[:], 1.0), writes=(("ones",),))
        S.op("dve", lambda e: e.memset(scr[:], 0.0), writes=(("scrz",),))
        S.op("dve", lambda e: e.memset(stout[:], 0.0), writes=(("stout",),))
        S.dma("sp", lambda e: e.dma_start(out=pv[:], in_=dr["pv"]), "pv", writes=(("pv",),))
        S.dma("sp", lambda e: e.dma_start(out=cvt[:], in_=dr["cv"]), "cv", writes=(("cvt",),))
        S.dma("sp", lambda e: e.dma_start(out=h0t[:], in_=dr["h0"]), "h0", writes=(("h0t",),))
        S.dma("sp", lambda e: e.dma_start(out=maskt[:], in_=dr["mask"]), "mask", writes=(("maskt",),))
        S.dma("pool", lambda e: e.dma_start(out=dsm[:], in_=dr["dsm"]), "dsm", writes=(("dsm",),))
        for kt in range(16):
            keys = tuple(("x", kt, c) for c in range(3))
            S.dma("sp", lambda e, kt=kt: e.dma_start(out=x[:, kt, :], in_=dr["xT"][:, kt, :]),
                  "x%d" % kt, writes=keys)
        for j_ in range(2):
            issue_block(j_)
        wstate["next_issue"] = 2
        for kt in range(16):
            keys = tuple(("x", kt, c) for c in range(2))
            S.dma("pool", lambda e, kt=kt: e.dma_start(out=x[:, kt, 0:1024], in_=dr["pos"][:, kt, :],
                                                        accum_op=ALU.add),
                  "xp%d" % kt, reads=keys, writes=keys)

        act(cvth[:], cvt[:], AF.Tanh, reads=(("cvt",),), writes=(("cvth",),), scale=0.5)
        stt(cvth[:], cvth[:], 1.0, cvt[:], ALU.add, ALU.mult, reads=(("cvth",), ("cvt",)), writes=(("cvth",),))
        ts(scv[:], cvth[:], 0.5, None, ALU.mult, None, reads=(("cvth",),), writes=(("scv",),))

        def mod_block(l, b):
            blk, wkey = wnext(("mod", l, b))
            bv = blk.rearrange("p (k n) -> p k n", k=16, n=256)
            for tl in range(2):
                ft = 2 * b + tl
                out = ps[:, 7, ft * 2:ft * 2 + 2]

                def fn(e, out=out, bv=bv, tl=tl):
                    ins = None
                    for kt in range(16):
                        ins = e.matmul(out, bv[:, kt, tl * 128:(tl + 1) * 128], scv[:, kt * 2:kt * 2 + 2],
                                       start=(kt == 0), stop=(kt == 15))
                    return ins
                S.op("pe", fn, reads=(wkey, ("scv",)), writes=(("psmod",),))

        def mod_finalize(l, half):
            pm = ps[:, 7, 0:192].rearrange("p (f j) -> p f j", f=96, j=2)
            bo = PV_OFF[("bmod", l)]
            f0, f1 = {0: (0, 32), 2: (32, 48), 1: (48, 80), 3: (80, 96)}[half]
            for j in range(2):
                tt(modv[:, l, f0:f1, j], pm[:, f0:f1, j], pv[:, bo + f0:bo + f1], ALU.add,
                   reads=(("psmod",), ("pv",)), writes=(("modv", l, half),))
            if half in (2, 3):
                return
            which, (gname, si) = half, (("nm", 1), ("nf", 4))[half]
            go = PV_OFF[(gname, l)]
            for j in range(2):
                stt(Amod[:, l, which, :, j], modv[:, l, si * 16:(si + 1) * 16, j], 1.0, pv[:, go:go + 16],
                    ALU.add, ALU.mult, reads=(("modv", l, half), ("pv",)), writes=(("Amod", l, half),))
            if half == 1:
                return
            o = PV_OFF[("ba", l)]
            ts(lrup[:, l, 0, :], pv[:, o:o + 16], 0.5, None, ALU.mult, None, reads=(("pv",),), writes=(("lrup", l),))
            o = PV_OFF[("bx", l)]
            ts(lrup[:, l, 1, :], pv[:, o:o + 16], 0.5, None, ALU.mult, None, reads=(("pv",),), writes=(("lrup", l),))
            o = PV_OFF[("lam", l)]
            act(lrup[:, l, 2, :], pv[:, o:o + 16], AF.Exp, reads=(("pv",),), writes=(("lrup", l),), scale=-1.0)
            act(lrup[:, l, 2, :], lrup[:, l, 2, :], AF.Ln, reads=(("lrup", l),), writes=(("lrup", l),),
                scale=1.0, bias=cst[:, 1:2])
            ts(lrup[:, l, 2, :], lrup[:, l, 2, :], -4.0, None, ALU.mult, None,
               reads=(("lrup", l),), writes=(("lrup", l),))

        def colsum_rstd(src_fn, ntiles, inv_n, src_reads_fn, tag):
            banks = []
            for c, (t0, W) in enumerate(CH):
                bank = S.bank()
                banks.append(bank)
                for i in range(ntiles):
                    sl = i % 2
                    if sl == 0:
                        act(sqb[:, sl, 0:W], src_fn(i, c), AF.Square, reads=src_reads_fn(i, c), writes=(("sqb", sl),))
                    else:
                        tt(sqb[:, sl, 0:W], src_fn(i, c), src_fn(i, c), ALU.mult, reads=src_reads_fn(i, c),
                           writes=(("sqb", sl),))
                    out = ps[:, bank, 0:W]
                    S.op("pe", lambda e, out=out, sl=sl, W=W, i=i: e.matmul(out, ones[:], sqb[:, sl, 0:W],
                                                                          start=(i == 0), stop=(i == ntiles - 1)),
                         reads=(("sqb", sl), ("ones",)), writes=(("ps", bank),))
            for c, (t0, W) in enumerate(CH):
                act(rstd[:, t0:t0 + W], ps[:, banks[c], 0:W], AF.Sqrt, reads=(("ps", banks[c]),), writes=(("rstd", c),),
                    scale=inv_n, bias=cst[:, 0:1])
            for c, (t0, W) in enumerate(CH):
                recip(rstd[:, t0:t0 + W], rstd[:, t0:t0 + W], reads=(("rstd", c),), writes=(("rstd", c),))

        def norm_stats():
            colsum_rstd(lambda kt, c: x[:, kt, CH[c][0]:CH[c][0] + CH[c][1]], 16, 1.0 / D,
                        lambda kt, c: (("x", kt, c),), "nm")

        def norm_mod(l, which, tmpf, stats=True):
            if stats:
                norm_stats()
            si = 0 if which == 0 else 3
            for c, (t0, W) in enumerate(CH):
                j = CJ[c]
                for kt in range(16):
                    sl = kt % 2
                    stt(tmpf[:, sl, 0:W], x[:, kt, t0:t0 + W], Amod[:, l, which, kt, j:j + 1], rstd[:, t0:t0 + W],
                        ALU.mult, ALU.mult, reads=(("x", kt, c), ("Amod", l, which), ("rstd", c)), writes=(("tmpf", l, which, sl),))
                    act(hb[:, kt, t0:t0 + W], tmpf[:, sl, 0:W], AF.Identity,
                        reads=(("tmpf", l, which, sl), ("modv", l, which)), writes=(("hb", kt, c),), scale=1.0, bias=modc(l, si, kt, j))

        def group_finish(l, mG, mkey, ntiles, gcol0, out_name, nblk, kt_per_blk_cols, tag, after_blk=None):
            colsum_rstd(lambda i, c: mG[:, i, CH[c][0]:CH[c][0] + CH[c][1]], ntiles, 1.0 / (ntiles * 128),
                        lambda i, c: ((mkey, i),), tag)
            for c, (t0, W) in enumerate(CH):
                for i in range(ntiles):
                    stt(mG[:, i, t0:t0 + W], mG[:, i, t0:t0 + W], pvc("on", l, gcol0 + i), rstd[:, t0:t0 + W],
                        ALU.mult, ALU.mult, reads=((mkey, i), ("pv",), ("rstd", c)), writes=((mkey, i, c),))
            ncols = BLK // ntiles
            nft = ncols // 128
            for q in range(nblk):
                blk, wkey = wnext((out_name, l, q))
                bv = blk.rearrange("p (k n) -> p k n", k=ntiles, n=ncols)
                for c, (t0, W) in enumerate(CH):
                    for fl in range(nft):
                        ft = q * nft + fl
                        bank = S.bank()
                        mm_group(bank, W, [(bv[:, kt, fl * 128:(fl + 1) * 128], mG[:, kt, t0:t0 + W])
                                           for kt in range(ntiles)],
                                 reads=(wkey,) + tuple((mkey, kt, c) for kt in range(ntiles)))
                        stt(x[:, ft, t0:t0 + W], ps[:, bank, 0:W], modc(l, 2, ft, CJ[c]), x[:, ft, t0:t0 + W],
                            ALU.mult, ALU.add, reads=(("ps", bank), ("modv", l, 2), ("x", ft, c)), writes=(("x", ft, c),))
                if after_blk is not None:
                    after_blk()

        def group_A(l):
            S.barrier()
            cv_ = Carve()
            mA = cv_.take([8, NT], BF16)
            xpad = cv_.take([NSEG, 286], BF16)
            dgA = cv_.take([4, 128], BF16)
            xc = cv_.take([NT], F32)
            xcb = cv_.take([NT], BF16)
            A1 = cv_.take([NT], F32)
            A2 = cv_.take([NT], F32)
            A3 = cv_.take([NT], F32)
            A4 = cv_.take([NT], F32)
            T = "A%d" % l
            ident = dsm[:, 1280:1408]
            mA32 = scr[:, 0:5120]
            RK = (("rstd", 0), ("rstd", 1), ("rstd", 2))
            S.op("dve", lambda e: e.memset(xpad[:, :, 0:2], 0.0), writes=((T, "xpad"),))
            S.op("dve", lambda e: e.memset(xpad[:, :, 258:286], 0.0), writes=((T, "xpad"),))
            def make_tile(n):
                st = {}
                def head():
                    blk, wkey = wnext(("inA", l, n))
                    bv = blk.rearrange("p (k n) -> p k n", k=16, n=256)
                    mod_after = [16 + 2 * n, 17 + 2 * n] if (l == 0 and n < 4) else []
                    lsl = n % 2
                    S.dma("pool", lambda e, lsl=lsl, n=n: e.dma_start(out=lruwb[:, lsl, :], in_=dr["lruw"][l * 8 + n]),
                          "lruw%d" % lsl, writes=(("lruw", lsl),))
                    lw = lruwb[:, lsl, :].rearrange("p (g j) -> p g j", g=4, j=128)
                    S.ring = [0, 1, 2, 3]
                    bya = [4, 5, 6]
                    for k in range(4):
                        act(dgA[:, k, :], ident, AF.Copy, reads=(("dsm",), ("pv",)), writes=((T, "dgA"),),
                            scale=pvc("caw", l, k * 8 + n))
                    bxa = []
                    for c, (t0, W) in enumerate(CH):
                        bank = S.bank()
                        bxa.append(bank)
                        mm_group(bank, W, [(bv[:, kt, 0:128], hb[:, kt, t0:t0 + W]) for kt in range(16)],
                                 reads=(wkey,) + tuple(("hb", kt, c) for kt in range(16)))
                    for c, (t0, W) in enumerate(CH):
                        g0, ng = CHSEG[c]
                        act(xpad[:, g0:g0 + ng, 2:258], ps[:, bxa[c], 0:W].rearrange("p (a b) -> p a b", a=ng, b=SEG),
                            AF.Copy, reads=(("ps", bxa[c]),), writes=((T, "xpad"),))
                    dbg("evac")
                    act(xpad[:, 1:4, 0:2], xpad[:, 0:3, 256:258], AF.Copy, reads=((T, "xpad"), ("maskt",)),
                        writes=((T, "xpad"),), scale=maskt[:, 0:1])
                    act(xpad[:, 0:3, 258:259], xpad[:, 1:4, 2:3], AF.Copy, reads=((T, "xpad"), ("maskt",)),
                        writes=((T, "xpad"),), scale=maskt[:, 0:1])
                    dbg("pads")
                    for c, (t0, W) in enumerate(CH):
                        g0, ng = CHSEG[c]
                        bank = S.bank()
                        mm_group(bank, W, [(dgA[:, k, :], xpad[:, g0:g0 + ng, k:k + 256]) for k in range(4)],
                                 reads=((T, "dgA"), (T, "xpad")))
                        act(xc[:, t0:t0 + W], ps[:, bank, 0:W], AF.Identity, reads=(("ps", bank), ("pv",)),
                            writes=((T, "xc"),), scale=1.0, bias=pvc("cab", l, n))
                        if os.environ.get("GA_VAR") == "noxcb":
                            continue
                        act(xcb[:, t0:t0 + W], ps[:, bank, 0:W], AF.Identity, reads=(("ps", bank), ("pv",)),
                            writes=((T, "xcb"),), scale=1.0, bias=pvc("cab", l, n))

                    def ya_matmuls():
                        for c, (t0, W) in enumerate(CH):
                            mm_group(4 + c, W, [(bv[:, kt, 128:256], hb[:, kt, t0:t0 + W]) for kt in range(16)],
                                     reads=(wkey,) + tuple(("hb", kt, c) for kt in range(16)))
                        for b_ in mod_after:
                            mod_block(0, b_)

                    st.update(dict(bv=bv, wkey=wkey, lsl=lsl, lw=lw, bya=bya, ya_matmuls=ya_matmuls))
                def main():
                    bv, wkey, lsl, lw, bya, ya_matmuls = (st[k] for k in ('bv', 'wkey', 'lsl', 'lw', 'bya', 'ya_matmuls'))
                    bufs = {0: dict(a=A1[:], ka=((T, "A1"),), s=A2[:], ks=((T, "A2"),), u=A3[:], ku=((T, "A3"),))}
                    if n <= 5:
                        u1 = mA32[:, (n + 1) * 640:(n + 3) * 640]
                        ku1 = ((T + "m", n + 1), (T + "m", n + 2))
                    else:
                        u1, ku1 = A3[:], ((T, "A3"),)
                    bufs[1] = dict(a=A4[:], ka=((T, "A4"),), s=rstd[:, :], ks=RK, u=u1, ku=ku1)
                    if n <= 3:
                        G = mA32[:, (n + 3) * 640:(n + 5) * 640]
                        kG = ((T + "m", n + 3), (T + "m", n + 4))
                    else:
                        G, kG = A1[:], ((T, "A1"),)

                    def lru_act(d, part=None):
                        B = bufs[d]
                        pcol = d * 8 + n
                        for c, (t0, W) in enumerate(CH):
                            if part not in (None, "tanh"):
                                break
                            b1 = S.bank()
                            mm_group(b1, W, [(lw[:, d, :], xcb[:, t0:t0 + W])], reads=(("lruw", lsl), (T, "xcb")))
                            b2 = S.bank()
                            mm_group(b2, W, [(lw[:, 2 + d, :], xcb[:, t0:t0 + W])], reads=(("lruw", lsl), (T, "xcb")))
                            act(B["a"][:, t0:t0 + W], ps[:, b1, 0:W], AF.Tanh, reads=(("ps", b1), ("lrup", l)),
                                writes=B["ka"], scale=0.5, bias=lrup[:, l, 0, pcol:pcol + 1])
                            act(B["u"][:, t0:t0 + W], ps[:, b2, 0:W], AF.Tanh, reads=(("ps", b2), ("lrup", l)),
                                writes=B["ku"], scale=0.5, bias=lrup[:, l, 1, pcol:pcol + 1])
                        if part in (None, "exp"):
                            act(B["a"], B["a"], AF.Exp, reads=B["ka"] + (("lrup", l),), writes=B["ka"],
                                scale=lrup[:, l, 2, pcol:pcol + 1], bias=lrup[:, l, 2, pcol:pcol + 1])
                            act(B["s"], B["a"], AF.Square, reads=B["ka"], writes=B["ks"])
                        if part in (None, "sqrt"):
                            act(B["s"], B["s"], AF.Sqrt, reads=B["ks"], writes=B["ks"], scale=-0.25, bias=cst[:, 3:4])

                    def lru_dve(d):
                        B = bufs[d]
                        stt(B["u"], B["u"], 1.0, B["s"], ALU.add, ALU.mult, reads=B["ku"] + B["ks"], writes=B["ku"])
                        tt(B["u"], B["u"], xc[:], ALU.mult, reads=B["ku"] + ((T, "xc"),), writes=B["ku"])
                        if d == 0:
                            bcols = B["a"][:, 256:1024:512]
                        else:
                            bcols = B["a"][:, 255:1023:512]
                        ts(bcols, bcols, maskt[:, 0:1], None, ALU.mult, None, reads=B["ka"] + (("maskt",),), writes=B["ka"])
                        hbase = ((l * 2 + d) * 8 + n) * NSEG
                        Hd = B["s"]
                        if d == 0:
                            plan_ = [(0, 512, 0, None), (512, 1024, 2, 511), (1024, 1280, 4, None)]
                        else:
                            plan_ = [(512, 1024, 3, None), (0, 512, 1, 512), (1024, 1280, 4, None)]
                        for (c0, c1, gi, carry) in plan_:
                            if carry is None:
                                init = h0t[:, hbase + gi:hbase + gi + 1]
                                rkeys = (("h0t",),)
                            else:
                                ib = d * 8 + gi
                                stt(initb[:, ib:ib + 1], Hd[:, carry:carry + 1], maskt[:, 0:1], h0t[:, hbase + gi:hbase + gi + 1],
                                    ALU.mult, ALU.add, reads=B["ks"] + (("maskt",), ("h0t",)), writes=(("initb", ib),))
                                init = initb[:, ib:ib + 1]
                                rkeys = (("initb", ib),)
                            if d == 0:
                                o_, a_, u_ = Hd[:, c0:c1], B["a"][:, c0:c1], B["u"][:, c0:c1]
                            else:
                                o_, a_, u_ = Hd[:, c0:c1][:, ::-1], B["a"][:, c0:c1][:, ::-1], B["u"][:, c0:c1][:, ::-1]
                            S.op("dve", lambda e, o_=o_, a_=a_, u_=u_, init=init: e.tensor_tensor_scan(
                                out=o_, data0=a_, data1=u_, initial=init, op0=ALU.mult, op1=ALU.add),
                                reads=B["ka"] + B["ku"] + rkeys, writes=B["ks"])
                        Hv = Hd.rearrange("p (g s) -> p g s", g=NSEG, s=SEG)
                        col = SEG - 1 if d == 0 else 0
                        S.op("dve", lambda e, Hv=Hv, col=col, hbase=hbase: e.tensor_copy(
                            out=stout[:, hbase:hbase + NSEG], in_=Hv[:, :, col]),
                            reads=B["ks"], writes=(("stout",),))

                    def gelu_sq():
                        for c, (t0, W) in enumerate(CH):
                            act(G[:, t0:t0 + W], ps[:, bya[c], 0:W], AF.Square, reads=(("ps", bya[c]),), writes=kG,
                                scale=0.21145921594497926)

                    def gelu_poly():
                        for c, (t0, W) in enumerate(CH):
                            stt(G[:, t0:t0 + W], G[:, t0:t0 + W], 1.0, ps[:, bya[c], 0:W], ALU.add, ALU.mult,
                                reads=kG + (("ps", bya[c]),), writes=kG)

                    def gelu_tanh():
                        for c, (t0, W) in enumerate(CH):
                            act(G[:, t0:t0 + W], G[:, t0:t0 + W], AF.Tanh, reads=kG, writes=kG, scale=0.7978845608028654)

                    def gelu_fin():
                        for c, (t0, W) in enumerate(CH):
                            yp = ps[:, bya[c], 0:W]
                            stt(G[:, t0:t0 + W], G[:, t0:t0 + W], 1.0, yp, ALU.add, ALU.mult,
                                reads=kG + (("ps", bya[c]),), writes=kG)

                    early_gelu = n <= 3
                    if n <= 5:
                        lru_act(0, "tanh")
                        lru_act(1, "tanh")
                        dbg("tanh")
                        ya_matmuls()
                        lru_act(0, "exp")
                        lru_act(1, "exp")
                        lru_act(0, "sqrt")
                        lru_act(1, "sqrt")
                        dbg("sqrt")
                        if early_gelu:
                            gelu_sq()
                        dbg("gsq")
                        lru_dve(0)
                        dbg("dve0")
                        if early_gelu:
                            gelu_poly()
                            gelu_tanh()
                        dbg("gpoly")
                        lru_dve(1)
                        dbg("dve1")
                    else:
                        lru_act(0)
                        ya_matmuls()
                        lru_dve(0)
                        lru_act(1)
                        lru_dve(1)
                    st.update(dict(gelu_sq=gelu_sq, gelu_poly=gelu_poly, gelu_tanh=gelu_tanh, gelu_fin=gelu_fin, G=G, kG=kG, early_gelu=early_gelu))
                def tail():
                    gelu_sq, gelu_poly, gelu_tanh, gelu_fin, G, kG, early_gelu = (st[k] for k in ('gelu_sq', 'gelu_poly', 'gelu_tanh', 'gelu_fin', 'G', 'kG', 'early_gelu'))
                    if not early_gelu:
                        gelu_sq()
                        gelu_poly()
                        gelu_tanh()
                    gelu_fin()
                    tt(A2[:], A2[:], rstd[:, :], ALU.add, reads=((T, "A2"),) + RK, writes=((T, "A2"),))
                    stt(mA[:, n, :], G, 0.5, A2[:], ALU.mult, ALU.mult, reads=kG + ((T, "A2"),),
                        writes=((T + "m", n),))
                return head, main, tail

            tiles = [make_tile(n) for n in range(8)]
            tiles[0][0]()
            for n in range(8):
                tiles[n][1]()
                if n + 1 < 8:
                    tiles[n + 1][0]()
                tiles[n][2]()
            S.ring = list(range(7))
            if l == 0:
                mod_finalize(l, 2)
            afq = [24]

            def afmods():
                for _ in range(2):
                    mod_block(0, afq[0])
                    afq[0] += 1
            if l == 0:
                S.barrier()
            group_finish(l, mA, T + "m", 8, 0, "outA", 4, None, T, after_blk=(afmods if l == 0 else None))

        def group_B(l):
            S.barrier()
            cv_ = Carve()
            Y = cv_.take([4, 10, 256], BF16)
            mB = cv_.take([4, NT], BF16)
            xbT = cv_.take([2, NT], BF16)
            T = "B%d" % l
            ccsc = dsm[:, 0:256]
            p2c = dsm[:, 256:768].rearrange("p (a b) -> p a b", a=2, b=256)
            p2s = dsm[:, 768:1280].rearrange("p (a b) -> p a b", a=2, b=256)
            bq = [32]

            def bmods():
                if l == 0:
                    mod_block(0, bq[0])
                    bq[0] += 1
            for b in range(2):
                blk, wkey = wnext(("inB", l, b))
                bv = blk.rearrange("p (k n) -> p k n", k=16, n=256)
                for tl in range(2):
                    for c, (t0, W) in enumerate(CH):
                        bank = S.bank()
                        mm_group(bank, W, [(bv[:, kt, tl * 128:(tl + 1) * 128], hb[:, kt, t0:t0 + W]) for kt in range(16)],
                                 reads=(wkey,) + tuple(("hb", kt, c) for kt in range(16)))
                        act(xbT[:, tl, t0:t0 + W], ps[:, bank, 0:W], AF.Copy, reads=(("ps", bank),),
                            writes=((T, "xbT", tl),))
                for tl in range(2):
                    tile_ = 2 * b + tl
                    for st in range(10):
                        bank = S.bank()
                        mm_group(bank, 256, [(xbT[:, tl, st * 128:(st + 1) * 128], ccsc)],
                                 reads=((T, "xbT", tl), ("dsm",)))
                        act(Y[:, tile_, st, :], ps[:, bank, 0:256], AF.Copy, reads=(("ps", bank),),
                            writes=((T, "Y", tile_),))
                bmods()
            for h in range(2):
                bc, kc = wnext(("P1c", h))
                bcv = bc.rearrange("p (k n) -> p k n", k=8, n=512)
                pbanks = [S.bank() for _ in range(4)]
                for tile_ in range(4):
                    mm_group(pbanks[tile_], 512, [(Y[:, tile_, st, 0:128], bcv[:, st, :]) for st in range(8)],
                             reads=(kc, (T, "Y", tile_)), first=True, last=False)
                bmods()
                bs, ks = wnext(("P1ns", h))
                bsv = bs.rearrange("p (k n) -> p k n", k=8, n=512)
                for tile_ in range(4):
                    bank = pbanks[tile_]
                    mm_group(bank, 512, [(Y[:, tile_, st, 128:256], bsv[:, st, :]) for st in range(8)],
                             reads=(ks, (T, "Y", tile_)), first=False, last=True)
                    act(mB[:, tile_, h * 512:(h + 1) * 512], ps[:, bank, 0:512], AF.Copy, reads=(("ps", bank),),
                        writes=((T + "m", tile_),))
                bmods()
            for tile_ in range(4):
                bank = S.bank()
                mms = []
                for st in range(2):
                    mms.append((Y[:, tile_, 8 + st, 0:128], p2c[:, st, :]))
                    mms.append((Y[:, tile_, 8 + st, 128:256], p2s[:, st, :]))
                mm_group(bank, 256, mms, reads=(("dsm",), (T, "Y", tile_)))
                act(mB[:, tile_, 1024:1280], ps[:, bank, 0:256], AF.Copy, reads=(("ps", bank),),
                    writes=((T + "m", tile_),))
            group_finish(l, mB, T + "m", 4, 8, "outB", 2, None, T, after_blk=bmods)

        def group_C(l):
            S.barrier()
            cv_ = Carve()
            vc = cv_.take([4, NT], F32)
            mC = cv_.take([4, NT], BF16)
            vpad = cv_.take([NSEG, 286], BF16)
            T1 = cv_.take([NT], F32)
            mean = cv_.take([NT], F32)
            dg = cv_.take([31, 128], BF16)
            T = "C%d" % l
            ident = dsm[:, 1280:1408]
            S.op("dve", lambda e: e.memset(vpad[:, :, 0:15], 0.0), writes=((T, "vpad"),))
            S.op("dve", lambda e: e.memset(vpad[:, :, 271:286], 0.0), writes=((T, "vpad"),))
            for j in range(4):
                blk, wkey = wnext(("inC", l, j))
                bv = blk.rearrange("p (k n) -> p k n", k=16, n=256)
                for k in range(31):
                    if k % 2 == 0:
                        ts(dg[:, k, :], ident, pvc("ccw", l, k * 4 + j), None, ALU.mult, None,
                           reads=(("dsm",), ("pv",)), writes=((T, "dg"),))
                    else:
                        act(dg[:, k, :], ident, AF.Copy, reads=(("dsm",), ("pv",)), writes=((T, "dg"),),
                            scale=pvc("ccw", l, k * 4 + j))
                for c, (t0, W) in enumerate(CH):
                    g0, ng = CHSEG[c]
                    bvv = S.bank()
                    mm_group(bvv, W, [(bv[:, kt, 0:128], hb[:, kt, t0:t0 + W]) for kt in range(16)],
                             reads=(wkey,) + tuple(("hb", kt, c) for kt in range(16)))
                    bg = S.bank()
                    mm_group(bg, W, [(bv[:, kt, 128:256], hb[:, kt, t0:t0 + W]) for kt in range(16)],
                             reads=(wkey,) + tuple(("hb", kt, c) for kt in range(16)))
                    act(T1[:, t0:t0 + W], ps[:, bg, 0:W], AF.Tanh, reads=(("ps", bg),), writes=((T, "T1"),), scale=0.5)
                    stt(T1[:, t0:t0 + W], T1[:, t0:t0 + W], 1.0, ps[:, bvv, 0:W], ALU.add, ALU.mult,
                        reads=((T, "T1"), ("ps", bvv)), writes=((T, "T1"),))
                    act(vpad[:, g0:g0 + ng, 15:271], T1[:, t0:t0 + W].rearrange("p (a b) -> p a b", a=ng, b=SEG),
                        AF.Copy, reads=((T, "T1"),), writes=((T, "vpad"),), scale=0.5)
                ts(vpad[:, 1:4, 0:15], vpad[:, 0:3, 256:271], maskt[:, 0:1], None, ALU.mult, None,
                   reads=((T, "vpad"), ("maskt",)), writes=((T, "vpad"),))
                ts(vpad[:, 0:3, 271:286], vpad[:, 1:4, 15:30], maskt[:, 0:1], None, ALU.mult, None,
                   reads=((T, "vpad"), ("maskt",)), writes=((T, "vpad"),))
                for c, (t0, W) in enumerate(CH):
                    g0, ng = CHSEG[c]
                    bank = S.bank()
                    mm_group(bank, W, [(dg[:, k, :], vpad[:, g0:g0 + ng, k:k + 256]) for k in range(31)],
                             reads=((T, "dg"), (T, "vpad")))
                    act(vc[:, j, t0:t0 + W], ps[:, bank, 0:W], AF.Identity, reads=(("ps", bank), ("pv",)),
                        writes=((T, "vc", j),), scale=1.0, bias=pvc("ccb", l, j))
            bm, be = [], []
            for c, (t0, W) in enumerate(CH):
                b1 = S.bank()
                b2 = S.bank()
                bm.append(b1)
                be.append(b2)
                for j in range(4):
                    act(sqb[:, 0, 0:W], vc[:, j, t0:t0 + W], AF.Copy, reads=((T, "vc", j),), writes=(("sqb", 0),))
                    S.op("pe", lambda e, b1=b1, W=W, j=j: e.matmul(ps[:, b1, 0:W], ones[:], sqb[:, 0, 0:W],
                                                                 start=(j == 0), stop=(j == 3)),
                         reads=(("sqb", 0), ("ones",)), writes=(("ps", b1),))
                    act(sqb[:, 1, 0:W], vc[:, j, t0:t0 + W], AF.Square, reads=((T, "vc", j),), writes=(("sqb", 1),))
                    S.op("pe", lambda e, b2=b2, W=W, j=j: e.matmul(ps[:, b2, 0:W], ones[:], sqb[:, 1, 0:W],
                                                                 start=(j == 0), stop=(j == 3)),
                         reads=(("sqb", 1), ("ones",)), writes=(("ps", b2),))
            for c, (t0, W) in enumerate(CH):
                ts(mean[:, t0:t0 + W], ps[:, bm[c], 0:W], 1.0 / D_C, None, ALU.mult, None,
                   reads=(("ps", bm[c]),), writes=((T, "mean"),))
                tt(T1[:, t0:t0 + W], mean[:, t0:t0 + W], mean[:, t0:t0 + W], ALU.mult,
                   reads=((T, "mean"),), writes=((T, "T1"),))
                stt(T1[:, t0:t0 + W], ps[:, be[c], 0:W], 1.0 / D_C, T1[:, t0:t0 + W], ALU.mult, ALU.subtract,
                    reads=(("ps", be[c]), (T, "T1")), writes=((T, "T1"),))
            act(T1[:], T1[:], AF.Sqrt, reads=((T, "T1"),), writes=((T, "T1"),), scale=1.0, bias=cst[:, 0:1])
            recip(rstd[:, :], T1[:], reads=((T, "T1"),), writes=(("rstd", 0), ("rstd", 1), ("rstd", 2)))
            rk = (("rstd", 0), ("rstd", 1), ("rstd", 2))
            for j in range(4):
                tt(vc[:, j, :], vc[:, j, :], mean[:], ALU.subtract, reads=((T, "vc", j), (T, "mean")), writes=((T, "vc", j),))
                tt(vc[:, j, :], vc[:, j, :], rstd[:, :], ALU.mult, reads=((T, "vc", j),) + rk, writes=((T, "vc", j),))
                act(mC[:, j, :], vc[:, j, :], AF.Silu, reads=((T, "vc", j), ("pv",)), writes=((T + "m", j),),
                    scale=pvc("lng", l, j), bias=pvc("lnb", l, j))
            group_finish(l, mC, T + "m", 4, 12, "outC", 2, None, T)

        def ffn(l):
            S.barrier()
            cv_ = Carve()
            actb = cv_.take([8, NT], BF16)
            th = cv_.take([2, 512], F32)
            tmpf = cv_.take([2, 512], F32)
            T = "F%d" % l
            norm_mod(l, 1, tmpf)
            for q in range(6):
                nk = 8 if q < 5 else 4
                for tl in range(nk):
                    t = q * 8 + tl
                    blk, wkey = wnext(("gu", l, t))
                    bv = blk.rearrange("p (k n) -> p k n", k=16, n=256)
                    for c, (t0, W) in enumerate(CH):
                        bg = S.bank()
                        mm_group(bg, W, [(bv[:, kt, 0:128], hb[:, kt, t0:t0 + W]) for kt in range(16)],
                                 reads=(wkey,) + tuple(("hb", kt, c) for kt in range(16)))
                        bu = S.bank()
                        mm_group(bu, W, [(bv[:, kt, 128:256], hb[:, kt, t0:t0 + W]) for kt in range(16)],
                                 reads=(wkey,) + tuple(("hb", kt, c) for kt in range(16)))
                        sl = c % 2
                        act(th[:, sl, 0:W], ps[:, bg, 0:W], AF.Tanh, reads=(("ps", bg),), writes=((T, "th", sl),), scale=0.5)
                        stt(th[:, sl, 0:W], th[:, sl, 0:W], 1.0, ps[:, bg, 0:W], ALU.add, ALU.mult,
                            reads=((T, "th", sl), ("ps", bg)), writes=((T, "th", sl),))
                        stt(actb[:, tl, t0:t0 + W], th[:, sl, 0:W], 0.5, ps[:, bu, 0:W], ALU.mult, ALU.mult,
                            reads=((T, "th", sl), ("ps", bu)), writes=((T, "act", tl, c),))
                    if l == 0:
                        if t < 8:
                            mod_block(0, 40 + t)
                            if t == 7:
                                mod_finalize(0, 3)
                        mod_block(1, t)
                        if t == NHT - 1:
                            for b in range(NHT, 48):
                                mod_block(1, b)
                for r in range(4):
                    blk, wkey = wnext(("down", l, q, r))
                    bv = blk.rearrange("p (k n) -> p k n", k=8, n=512)
                    for fl in range(4):
                        ft = r * 4 + fl
                        for c, (t0, W) in enumerate(CH):
                            bank = S.bank()
                            mm_group(bank, W, [(bv[:, kt, fl * 128:(fl + 1) * 128], actb[:, kt, t0:t0 + W])
                                               for kt in range(nk)],
                                     reads=(wkey,) + tuple((T, "act", kt, c) for kt in range(nk)))
                            stt(x[:, ft, t0:t0 + W], ps[:, bank, 0:W], modc(l, 5, ft, CJ[c]), x[:, ft, t0:t0 + W],
                                ALU.mult, ALU.add, reads=(("ps", bank), ("modv", l, 3), ("x", ft, c)), writes=(("x", ft, c),))

        def finish(final_norm=True):
            S.barrier()
            if final_norm:
                colsum_rstd(lambda kt, c: x[:, kt, CH[c][0]:CH[c][0] + CH[c][1]], 16, 1.0 / D,
                            lambda kt, c: (("x", kt, c),), "fin")
            fo = PV_OFF[("fn", 0)]
            for kt in range(16):
                if final_norm:
                    for c, (t0, W) in enumerate(CH):
                        stt(x[:, kt, t0:t0 + W], x[:, kt, t0:t0 + W], pv[:, fo + kt:fo + kt + 1], rstd[:, t0:t0 + W],
                            ALU.mult, ALU.mult, reads=(("x", kt, c), ("pv",), ("rstd", c)), writes=(("x", kt, c),))
                S.dma("sp", lambda e, kt=kt: e.dma_start(out=dr["yT"][:, kt, :], in_=x[:, kt, :]),
                      "out", reads=tuple(("x", kt, c) for c in range(3)), writes=(("yout", kt),))
            S.dma("sp", lambda e: e.dma_start(out=dr["st"], in_=stout[:]), "out", reads=(("stout",),),
                  writes=(("stdone",),))
            S.emit([(k, v) for k, v in S.dmacount.items()])

        def staged():
            if stop == "load":
                return False
            norm_stats()
            for b in range(16):
                mod_block(0, b)
            for l in range(L):
                mod_finalize(l, 0)
                if l == 1:
                    mod_finalize(l, 2)
                    mod_finalize(l, 1)
                    mod_finalize(l, 3)
                S.barrier()
                norm_mod(l, 0, Carve().take([2, 512], F32), stats=(l > 0))
                if stop == "nm%d" % l:
                    for kt in range(16):
                        for c, (t0, W) in enumerate(CH):
                            act(x[:, kt, t0:t0 + W], hb[:, kt, t0:t0 + W], AF.Copy, reads=(("hb", kt, c),),
                                writes=(("x", kt, c),))
                    return False
                try:
                    group_A(l)
                except StopBuild:
                    return False
                if stop == "A%d" % l:
                    return False
                group_B(l)
                if stop == "B%d" % l:
                    return False
                group_C(l)
                if stop == "C%d" % l:
                    return False
                if l == 0:
                    mod_finalize(l, 1)
                ffn(l)
                if stop == "F%d" % l:
                    return False
            assert wstate["next_use"] == NB, (wstate, NB)
            return True

        finish(final_norm=staged())
    return nc


def _fm(v, ntile):
    return np.ascontiguousarray(np.asarray(v, np.float32).reshape(ntile, 128).T)


def _pos_embed():
    t = np.arange(1024)
    r = (t // 64).astype(np.float32)
    col = (t % 64).astype(np.float32)
    nf = D // 4
    omega = (1.0 / (np.float32(10000.0) ** (np.arange(nf, dtype=np.float32) / np.float32(nf)))).astype(np.float32)

    def enc(p):
        ang = (p[:, None] * omega[None, :]).astype(np.float32)
        return np.concatenate([np.sin(ang), np.cos(ang)], axis=-1)
    return np.concatenate([enc(r), enc(col)], axis=-1).astype(np.float32)


def _kblock(w, kt):
    ncols = w.shape[1]
    return w.reshape(kt, 128, ncols).transpose(1, 0, 2).reshape(128, kt * ncols)


def _dft_mats():
    def cs(n):
        k = np.arange(n, dtype=np.float64)
        ang = 2.0 * np.pi * np.outer(k, k) / n
        return np.cos(ang), np.sin(ang)
    c128, s128 = cs(128)
    c1024, s1024 = cs(1024)
    c256, s256 = cs(256)
    sc1 = 1.0 / math.sqrt(1024 * 128)
    sc2 = 1.0 / math.sqrt(256 * 128)
    full_c = (c1024 * sc1).astype(np.float32)
    full_ns = (-s1024 * sc1).astype(np.float32)
    blk_c = np.zeros((1024, 1024), np.float32)
    blk_ns = np.zeros((1024, 1024), np.float32)
    for g in range(4):
        blk_c[g * 256:(g + 1) * 256, g * 256:(g + 1) * 256] = (c256 * sc2)
        blk_ns[g * 256:(g + 1) * 256, g * 256:(g + 1) * 256] = (-s256 * sc2)

    def pblocks(pc, pns):
        out = np.zeros((4, 128, BLK), np.float32)
        for h in range(2):
            out[2 * h + 0] = _kblock(pc[:, h * 512:(h + 1) * 512], 8)
            out[2 * h + 1] = _kblock(pns[:, h * 512:(h + 1) * 512], 8)
        return out
    dsm = np.zeros((128, 1408), np.float32)
    dsm[:, 1280:1408] = np.eye(128, dtype=np.float32)
    dsm[:, 0:128] = c128
    dsm[:, 128:256] = s128
    dsm[:, 256:768] = _kblock((c256 * sc2).astype(np.float32), 2)
    dsm[:, 768:1280] = _kblock((-s256 * sc2).astype(np.float32), 2)
    return pblocks(full_c, full_ns), pblocks(blk_c, blk_ns), dsm


_CACHE = {}


def prepare_inputs(x_prompt, x_sample, c, state_lru, c_ctx, w_mod, b_mod, norm_mix, norm_ffn, w_in,
                   conv_a_w, conv_a_b, lru_wa, lru_ba, lru_wx, lru_bx, lru_lam, conv_c_w, conv_c_b,
                   ln_c_g, ln_c_b, out_norm, w_out, w_gu, w_down, final_norm):
    f32 = np.float32
    x_prompt = np.asarray(x_prompt, f32)
    x_sample = np.asarray(x_sample, f32)
    c = np.asarray(c, f32)
    state_lru = np.asarray(state_lru, f32)
    c_ctx = np.asarray(c_ctx, f32)
    w_mod = np.asarray(w_mod, f32); b_mod = np.asarray(b_mod, f32)
    w_in = np.asarray(w_in, f32); w_out = np.asarray(w_out, f32)
    w_gu = np.asarray(w_gu, f32); w_down = np.asarray(w_down, f32)
    lru_wa = np.asarray(lru_wa, f32); lru_wx = np.asarray(lru_wx, f32)

    pv = np.zeros((128, NPV), f32)

    def put(name, l, arr):
        o = PV_OFF[(name, l)]
        pv[:, o:o + arr.shape[1]] = arr
    for l in range(L):
        put("nm", l, _fm(norm_mix[l], 16))
        put("nf", l, _fm(norm_ffn[l], 16))
        caw = np.asarray(conv_a_w[l], f32)
        put("caw", l, np.concatenate([_fm(caw[k], 8) for k in range(4)], axis=1))
        put("cab", l, _fm(conv_a_b[l], 8))
        put("ba", l, np.concatenate([_fm(np.asarray(lru_ba)[l, d], 8) for d in range(2)], axis=1))
        put("bx", l, np.concatenate([_fm(np.asarray(lru_bx)[l, d], 8) for d in range(2)], axis=1))
        put("lam", l, np.concatenate([_fm(np.asarray(lru_lam)[l, d], 8) for d in range(2)], axis=1))
        ccw = np.asarray(conv_c_w[l], f32)
        put("ccw", l, np.concatenate([_fm(ccw[k], 4) for k in range(31)], axis=1))
        put("ccb", l, _fm(conv_c_b[l], 4))
        put("lng", l, _fm(ln_c_g[l], 4))
        put("lnb", l, _fm(ln_c_b[l], 4))
        put("on", l, _fm(out_norm[l], 16))
        put("bmod", l, _fm(b_mod[l], 96))
    put("fn", 0, _fm(final_norm, 16))

    blocks, _ = plan_phased()
    main = [b for b in blocks if b[0] not in ("P1c", "P1ns")]
    ws = np.zeros((len(main), 128, BLK), f32)
    for i, b in enumerate(main):
        kind = b[0]
        if kind == "mod":
            _, l, bb = b
            ws[i] = _kblock(w_mod[l][:, bb * 256:(bb + 1) * 256], 16)
        elif kind == "inA":
            _, l, n = b
            w = np.concatenate([w_in[l][:, n * 128:(n + 1) * 128], w_in[l][:, 1024 + n * 128:1024 + (n + 1) * 128]], axis=1)
            ws[i] = _kblock(w, 16)
        elif kind == "inB":
            _, l, bb = b
            ws[i] = _kblock(w_in[l][:, 2048 + bb * 256:2048 + (bb + 1) * 256], 16)
        elif kind == "inC":
            _, l, j = b
            w = np.concatenate([w_in[l][:, 2560 + j * 128:2560 + (j + 1) * 128],
                                w_in[l][:, 3072 + j * 128:3072 + (j + 1) * 128]], axis=1)
            ws[i] = _kblock(w, 16)
        elif kind == "outA":
            _, l, q = b
            ws[i] = _kblock(w_out[l][0:1024, q * 512:(q + 1) * 512], 8)
        elif kind == "outB":
            _, l, q = b
            ws[i] = _kblock(w_out[l][1024:1536, q * 1024:(q + 1) * 1024], 4)
        elif kind == "outC":
            _, l, q = b
            ws[i] = _kblock(w_out[l][1536:2048, q * 1024:(q + 1) * 1024], 4)
        elif kind == "gu":
            _, l, t = b
            w = np.concatenate([w_gu[l][:, t * 128:(t + 1) * 128], w_gu[l][:, D_FF + t * 128:D_FF + (t + 1) * 128]], axis=1)
            ws[i] = _kblock(w, 16)
        elif kind == "down":
            _, l, q, r = b
            nk = 8 if q < 5 else 4
            ws[i][:, :nk * 512] = _kblock(w_down[l][q * 1024:q * 1024 + nk * 128, r * 512:(r + 1) * 512], nk)
        else:
            raise AssertionError(kind)
    lruw = np.zeros((L * 8, 128, 512), f32)
    for l in range(L):
        for n in range(8):
            lruw[l * 8 + n] = np.concatenate([lru_wa[l, 0, n], lru_wa[l, 1, n], lru_wx[l, 0, n], lru_wx[l, 1, n]], axis=1)
    p_full, p_blk, dsm = _dft_mats()
    pos_fm = np.ascontiguousarray(_pos_embed().reshape(1024, 16, 128).transpose(2, 1, 0))
    pos_zero = np.zeros_like(pos_fm)

    in_maps = []
    for core in range(8):
        if core < 2:
            X = np.concatenate([x_sample[core], x_prompt[core]], axis=0)
            cv0 = c[core]
        else:
            X = np.concatenate([x_prompt[2 + 5 * (core - 2) + i] for i in range(5)], axis=0)
            cv0 = c_ctx
        xT = np.ascontiguousarray(X.reshape(NT, 16, 128).transpose(2, 1, 0))
        cv = np.stack([_fm(cv0, 16), _fm(c_ctx, 16)], axis=-1).reshape(128, 32)
        h0 = np.zeros((128, L, 2, 8, NSEG), f32)
        if core < 2:
            for l in range(L):
                h0[:, l, 0, :, 0] = _fm(state_lru[core, l, 0], 8)
                h0[:, l, 1, :, 3] = _fm(state_lru[core, l, 1], 8)
        mask = np.full((128, 1), 1.0 if core < 2 else 0.0, f32)
        in_maps.append({
            "xT": xT, "pos": pos_fm if core < 2 else pos_zero, "cv": np.ascontiguousarray(cv),
            "h0": np.ascontiguousarray(h0.reshape(128, -1)), "mask": mask, "pv": pv, "ws": ws,
            "pstr": p_full if core < 2 else p_blk, "dsm": dsm, "lruw": lruw,
        })

    return in_maps


def kernel(**inputs):
    f32 = np.float32
    in_maps = prepare_inputs(**inputs)
    if "nc" not in _CACHE:
        _CACHE["nc"] = build_program()
    nc = _CACHE["nc"]
    res = run_bass_kernel_spmd(nc, in_maps, core_ids=list(range(8)))
    return assemble(res.results)


def assemble(results):
    f32 = np.float32
    y_prompt = np.zeros((32, 256, D), f32)
    y_sample = np.zeros((2, 1024, D), f32)
    new_state = np.zeros((32, L, 2, D_A), f32)
    for core in range(8):
        r = results[core]
        Y = np.asarray(r["yT"], f32).transpose(2, 1, 0).reshape(NT, D)
        st = np.asarray(r["st"], f32).reshape(128, L, 2, 8, NSEG)
        stg = st.transpose(4, 1, 2, 3, 0).reshape(NSEG, L, 2, D_A)
        if core < 2:
            y_sample[core] = Y[0:1024]
            y_prompt[core] = Y[1024:1280]
            new_state[core] = stg[4]
        else:
            for i in range(5):
                bidx = 2 + 5 * (core - 2) + i
                y_prompt[bidx] = Y[i * 256:(i + 1) * 256]
                new_state[bidx] = stg[i]
    return (y_prompt, y_sample, new_state)
```
